# Optimizing a Trainium2 kernel written in Bass

```python
import math
import jax
import jax.numpy as jnp
from jax import lax
import numpy as np

D_MODEL = 1024
BATCH = 4
SEQ = 4096
DEPTH = 2

MLSTM_HEADS = 4
MLSTM_WIDTH = 768
MLSTM_HD = MLSTM_WIDTH // MLSTM_HEADS
MLSTM_CONV = 4
GLA_HEADS = 4
GLA_WIDTH = 768
GLA_KEY_WIDTH = GLA_WIDTH // 2
GLA_DK = GLA_KEY_WIDTH // GLA_HEADS
GLA_DV = GLA_WIDTH // GLA_HEADS
GLA_GATE_RANK = 16
GLA_GATE_TAU = 16.0
S5_WIDTH = 512
S5_GROUP = 16
S5_GROUPS = S5_WIDTH // S5_GROUP
S5_STATE = 64
DT_MIN = 1e-3
DT_MAX = 1e-1
CHUNK = 64
EPS = 1e-6
IN_SIZES = (MLSTM_WIDTH, MLSTM_WIDTH, MLSTM_WIDTH, MLSTM_WIDTH, MLSTM_HEADS, MLSTM_HEADS, MLSTM_WIDTH,
            GLA_KEY_WIDTH, GLA_KEY_WIDTH, GLA_WIDTH, GLA_GATE_RANK, GLA_WIDTH,
            S5_WIDTH, S5_WIDTH,
            3 * D_MODEL)
IN_WIDTH = sum(IN_SIZES)

kernel_name = 'hybrid_mlstm_gla_s5_gated_block'


def rmsnorm(x, w):
    xf = x.astype(jnp.float32)
    y = xf * lax.rsqrt(jnp.mean(xf * xf, axis=-1, keepdims=True) + EPS)
    return (y * w.astype(jnp.float32)).astype(x.dtype)


def head_norm(x, w, n_heads, center):
    b, l, width = x.shape
    xh = x.astype(jnp.float32).reshape(b, l, n_heads, width // n_heads)
    if center:
        xh = xh - jnp.mean(xh, axis=-1, keepdims=True)
    xh = xh * lax.rsqrt(jnp.mean(xh * xh, axis=-1, keepdims=True) + EPS)
    return xh.reshape(b, l, width) * w.astype(jnp.float32)


def causal_dwconv(x, w):
    taps = w.shape[0]
    l = x.shape[1]
    xp = jnp.pad(x, ((0, 0), (taps - 1, 0), (0, 0)))
    return sum(xp[:, j:j + l] * w[j] for j in range(taps))


def to_chunks(x, n_heads):
    b, l, width = x.shape
    x = x.reshape(b, l // CHUNK, CHUNK, n_heads, width // n_heads)
    return x.transpose(1, 0, 3, 2, 4)


def from_chunks(y):
    n, b, h, t, d = y.shape
    return y.transpose(1, 0, 3, 2, 4).reshape(b, n * t, h * d)


def gate_chunks(g):
    b, l, h = g.shape
    return g.reshape(b, l // CHUNK, CHUNK, h).transpose(1, 0, 3, 2)


def mlstm_chunkwise(q, k, v, i_pre, logf):
    b, l, _ = q.shape
    h, d = MLSTM_HEADS, MLSTM_HD
    qc = to_chunks(q, h)
    kc = to_chunks(k * d ** -0.5, h)
    vc = to_chunks(v, h)
    ic = gate_chunks(i_pre)
    fc = gate_chunks(logf)
    causal = jnp.tril(jnp.ones((CHUNK, CHUNK), dtype=bool))

    def step(carry, inp):
        c_mat, n_vec, m = carry
        qt, kt, vt, it, ft = inp
        bcum = jnp.cumsum(ft, axis=-1)
        dmat = bcum[..., :, None] - bcum[..., None, :] + it[..., None, :]
        dmat = jnp.where(causal, dmat, -jnp.inf)
        inter = bcum + m[..., None]
        m_t = jnp.maximum(inter, jnp.max(dmat, axis=-1))
        w_intra = jnp.exp(dmat - m_t[..., None])
        w_inter = jnp.exp(inter - m_t)
        s = jnp.einsum('bhtd,bhsd->bhts', qt, kt) * w_intra
        num = (jnp.einsum('bhts,bhse->bhte', s, vt)
               + w_inter[..., None] * jnp.einsum('bhtd,bhde->bhte', qt, c_mat))
        den = jnp.sum(s, axis=-1) + w_inter * jnp.einsum('bhtd,bhd->bht', qt, n_vec)
        out = num / jnp.maximum(jnp.abs(den), jnp.exp(-m_t))[..., None]
        b_end = bcum[..., -1]
        to_end = b_end[..., None] - bcum + it
        m_new = jnp.maximum(b_end + m, jnp.max(to_end, axis=-1))
        wk = jnp.exp(to_end - m_new[..., None])
        w_prev = jnp.exp(b_end + m - m_new)
        c_new = w_prev[..., None, None] * c_mat + jnp.einsum('bhs,bhsd,bhse->bhde', wk, kt, vt)
        n_new = w_prev[..., None] * n_vec + jnp.einsum('bhs,bhsd->bhd', wk, kt)
        return (c_new, n_new, m_new), out

    init = (jnp.zeros((b, h, d, d), jnp.float32), jnp.zeros((b, h, d), jnp.float32),
            jnp.zeros((b, h), jnp.float32))
    _, ys = lax.scan(step, init, (qc, kc, vc, ic, fc))
    return from_chunks(ys)


def gla_chunkwise(q, k, v, log_a):
    b, l, _ = q.shape
    h = GLA_HEADS
    qc = to_chunks(q * GLA_DK ** -0.5, h)
    kc = to_chunks(k, h)
    vc = to_chunks(v, h)
    ac = to_chunks(log_a, h)
    causal = jnp.tril(jnp.ones((CHUNK, CHUNK), dtype=bool))[:, :, None]

    def step(state, inp):
        qt, kt, vt, at = inp
        bcum = jnp.cumsum(at, axis=2)
        diff = bcum[:, :, :, None, :] - bcum[:, :, None, :, :]
        decay = jnp.exp(jnp.where(causal, diff, -jnp.inf))
        attn = jnp.einsum('bhtd,bhsd,bhtsd->bhts', qt, kt, decay)
        out = (jnp.einsum('bhts,bhse->bhte', attn, vt)
               + jnp.einsum('bhtd,bhde->bhte', qt * jnp.exp(bcum), state))
        last = bcum[:, :, -1:, :]
        new_state = (jnp.exp(last[:, :, 0])[..., None] * state
                     + jnp.einsum('bhsd,bhse->bhde', kt * jnp.exp(last - bcum), vt))
        return new_state, out

    init = jnp.zeros((b, h, GLA_DK, GLA_DV), jnp.float32)
    _, ys = lax.scan(step, init, (qc, kc, vc, ac))
    return from_chunks(ys)


def s5_ssm(u, lam_re, lam_im, log_dt, b_re, b_im, c_re, c_im, d_skip):
    f32 = jnp.float32
    bsz, l, _ = u.shape
    ug = u.reshape(bsz, l, S5_GROUPS, S5_GROUP)
    lr = jnp.minimum(lam_re.astype(f32), -1e-4)
    li = lam_im.astype(f32)
    dt = jnp.exp(log_dt.astype(f32))[:, None]
    mag = jnp.exp(lr * dt)
    ab_re = mag * jnp.cos(li * dt)
    ab_im = mag * jnp.sin(li * dt)
    nr = ab_re - 1.0
    den = lr * lr + li * li
    coef_re = (nr * lr + ab_im * li) / den
    coef_im = (ab_im * lr - nr * li) / den
    br, bi = b_re.astype(f32), b_im.astype(f32)
    bb_re = coef_re[..., None] * br - coef_im[..., None] * bi
    bb_im = coef_re[..., None] * bi + coef_im[..., None] * br
    bu_re = jnp.einsum('blgh,gph->blgp', ug, bb_re)
    bu_im = jnp.einsum('blgh,gph->blgp', ug, bb_im)
    a_re = jnp.broadcast_to(ab_re, bu_re.shape)
    a_im = jnp.broadcast_to(ab_im, bu_re.shape)

    def combine(e1, e2):
        a1r, a1i, b1r, b1i = e1
        a2r, a2i, b2r, b2i = e2
        return (a2r * a1r - a2i * a1i, a2r * a1i + a2i * a1r,
                a2r * b1r - a2i * b1i + b2r, a2r * b1i + a2i * b1r + b2i)

    _, _, x_re, x_im = lax.associative_scan(combine, (a_re, a_im, bu_re, bu_im), axis=1)
    y = (jnp.einsum('blgp,ghp->blgh', x_re, c_re.astype(f32))
         - jnp.einsum('blgp,ghp->blgh', x_im, c_im.astype(f32)))
    return y.reshape(bsz, l, S5_WIDTH) + d_skip.astype(f32) * u


def hybrid_layer(x, norm_w, w_in, mlstm_conv, mlstm_gate_b, mlstm_norm, gla_w_alpha, gla_b_alpha,
                 gla_norm, s5_lam_re, s5_lam_im, s5_log_dt, s5_B_re, s5_B_im, s5_C_re, s5_C_im,
                 s5_D, s5_w_glu, w_branch_mlstm, w_branch_gla, w_branch_s5, w_out):
    f32 = jnp.float32
    dt = x.dtype
    hn = rmsnorm(x, norm_w)
    proj = hn @ w_in
    split_at = np.cumsum(IN_SIZES)[:-1].tolist()
    (aq, ak, av, ao, ai, af, az, bq, bk, bv, ba, bz, cu, cz, g) = jnp.split(proj, split_at, axis=-1)

    qk = jax.nn.silu(causal_dwconv(jnp.concatenate([aq, ak], axis=-1), mlstm_conv))
    aq, ak = jnp.split(qk, 2, axis=-1)
    i_pre = (ai + mlstm_gate_b[0]).astype(f32)
    logf = jax.nn.log_sigmoid((af + mlstm_gate_b[1]).astype(f32))
    ha = mlstm_chunkwise(aq.astype(f32), ak.astype(f32), av.astype(f32), i_pre, logf)
    ha = head_norm(ha, mlstm_norm, MLSTM_HEADS, True).astype(dt)
    ya = jax.nn.sigmoid(ao) * ha * jax.nn.silu(az)

    log_a = jax.nn.log_sigmoid((ba @ gla_w_alpha + gla_b_alpha).astype(f32)) / GLA_GATE_TAU
    hb = gla_chunkwise(bq.astype(f32), bk.astype(f32), bv.astype(f32), log_a)
    yb = head_norm(hb, gla_norm, GLA_HEADS, False).astype(dt) * jax.nn.silu(bz)

    yc = jax.nn.gelu(s5_ssm(cu.astype(f32), s5_lam_re, s5_lam_im, s5_log_dt, s5_B_re, s5_B_im,
                            s5_C_re, s5_C_im, s5_D)).astype(dt)
    yc = yc * jax.nn.sigmoid(yc @ s5_w_glu) * jax.nn.silu(cz)

    ga, gb, gc = jnp.split(jax.nn.sigmoid(g), 3, axis=-1)
    merged = ga * (ya @ w_branch_mlstm) + gb * (yb @ w_branch_gla) + gc * (yc @ w_branch_s5)
    return x + merged @ w_out


def setup_inputs(seed: int = 0) -> dict:
    key = jax.random.key(seed)
    ks = jax.random.split(key, 24)

    def nrm(k, shape, scale):
        return scale * jax.random.normal(k, shape, jnp.float32)

    x = nrm(ks[0], (BATCH, SEQ, D_MODEL), 1.0)
    norm_w = 1.0 + nrm(ks[1], (DEPTH, D_MODEL), 0.02)
    w_in = nrm(ks[2], (DEPTH, D_MODEL, IN_WIDTH), D_MODEL ** -0.5)
    mlstm_conv = nrm(ks[3], (DEPTH, MLSTM_CONV, 2 * MLSTM_WIDTH), MLSTM_CONV ** -0.5)
    f_bias = jnp.linspace(3.0, 6.0, MLSTM_HEADS, dtype=jnp.float32)
    mlstm_gate_b = jnp.stack([nrm(ks[4], (DEPTH, MLSTM_HEADS), 0.1),
                              f_bias + nrm(ks[5], (DEPTH, MLSTM_HEADS), 0.1)], axis=1)
    mlstm_norm = 1.0 + nrm(ks[6], (DEPTH, MLSTM_WIDTH), 0.02)
    gla_w_alpha = nrm(ks[7], (DEPTH, GLA_GATE_RANK, GLA_KEY_WIDTH), GLA_GATE_RANK ** -0.5)
    gla_b_alpha = nrm(ks[8], (DEPTH, GLA_KEY_WIDTH), 0.1)
    gla_norm = 1.0 + nrm(ks[9], (DEPTH, GLA_WIDTH), 0.02)
    s5_lam_re = -0.5 + nrm(ks[10], (DEPTH, S5_GROUPS, S5_STATE), 0.01)
    s5_lam_im = (math.pi * jnp.arange(S5_STATE, dtype=jnp.float32)
                 + nrm(ks[11], (DEPTH, S5_GROUPS, S5_STATE), 0.01))
    s5_log_dt = jax.random.uniform(ks[12], (DEPTH, S5_GROUPS), jnp.float32,
                                   math.log(DT_MIN), math.log(DT_MAX))
    s5_B_re = nrm(ks[13], (DEPTH, S5_GROUPS, S5_STATE, S5_GROUP), (2 * S5_GROUP) ** -0.5)
    s5_B_im = nrm(ks[14], (DEPTH, S5_GROUPS, S5_STATE, S5_GROUP), (2 * S5_GROUP) ** -0.5)
    s5_C_re = nrm(ks[15], (DEPTH, S5_GROUPS, S5_GROUP, S5_STATE), S5_STATE ** -0.5)
    s5_C_im = nrm(ks[16], (DEPTH, S5_GROUPS, S5_GROUP, S5_STATE), S5_STATE ** -0.5)
    s5_D = nrm(ks[17], (DEPTH, S5_WIDTH), 0.5)
    s5_w_glu = nrm(ks[18], (DEPTH, S5_WIDTH, S5_WIDTH), S5_WIDTH ** -0.5)
    w_branch_mlstm = nrm(ks[19], (DEPTH, MLSTM_WIDTH, D_MODEL), MLSTM_WIDTH ** -0.5)
    w_branch_gla = nrm(ks[20], (DEPTH, GLA_WIDTH, D_MODEL), GLA_WIDTH ** -0.5)
    w_branch_s5 = nrm(ks[21], (DEPTH, S5_WIDTH, D_MODEL), S5_WIDTH ** -0.5)
    w_out = nrm(ks[22], (DEPTH, D_MODEL, D_MODEL), 0.5 * D_MODEL ** -0.5)
    final_norm = 1.0 + nrm(ks[23], (D_MODEL,), 0.02)
    return {'x': x, 'norm_w': norm_w, 'w_in': w_in, 'mlstm_conv': mlstm_conv,
            'mlstm_gate_b': mlstm_gate_b, 'mlstm_norm': mlstm_norm, 'gla_w_alpha': gla_w_alpha,
            'gla_b_alpha': gla_b_alpha, 'gla_norm': gla_norm, 's5_lam_re': s5_lam_re,
            's5_lam_im': s5_lam_im, 's5_log_dt': s5_log_dt, 's5_B_re': s5_B_re, 's5_B_im': s5_B_im,
            's5_C_re': s5_C_re, 's5_C_im': s5_C_im, 's5_D': s5_D, 's5_w_glu': s5_w_glu,
            'w_branch_mlstm': w_branch_mlstm, 'w_branch_gla': w_branch_gla,
            'w_branch_s5': w_branch_s5, 'w_out': w_out, 'final_norm': final_norm}


def reference(x, norm_w, w_in, mlstm_conv, mlstm_gate_b, mlstm_norm, gla_w_alpha, gla_b_alpha,
              gla_norm, s5_lam_re, s5_lam_im, s5_log_dt, s5_B_re, s5_B_im, s5_C_re, s5_C_im,
              s5_D, s5_w_glu, w_branch_mlstm, w_branch_gla, w_branch_s5, w_out, final_norm):
    for layer in range(DEPTH):
        x = hybrid_layer(x, norm_w[layer], w_in[layer], mlstm_conv[layer], mlstm_gate_b[layer],
                         mlstm_norm[layer], gla_w_alpha[layer], gla_b_alpha[layer], gla_norm[layer],
                         s5_lam_re[layer], s5_lam_im[layer], s5_log_dt[layer], s5_B_re[layer],
                         s5_B_im[layer], s5_C_re[layer], s5_C_im[layer], s5_D[layer],
                         s5_w_glu[layer], w_branch_mlstm[layer], w_branch_gla[layer],
                         w_branch_s5[layer], w_out[layer])
    return rmsnorm(x, final_norm)
```

```python
import numpy as np
from contextlib import ExitStack
import concourse.bass as bass
import concourse.mybir as mybir
from concourse.bass_utils import run_bass_kernel_spmd

F32 = mybir.dt.float32
BF16 = mybir.dt.bfloat16
ALU = mybir.AluOpType
AF = mybir.ActivationFunctionType

D = 1024
INW = 10264
NDS = 24
NDS_SW = 8


class Buf:
    __slots__ = ("w", "r", "name")

    def __init__(self, name=""):
        self.w = None
        self.r = {}
        self.name = name


class K:
    def __init__(self, nc, es, strict=True):
        self.nc = nc
        self.es = es
        self.strict = strict
        self.eng = {"pe": nc.tensor, "act": nc.scalar, "dve": nc.vector, "pool": nc.gpsimd, "sp": nc.sync}
        self.semobjs = []
        self.esem = {}
        for e in ["pe", "act", "dve", "pool"]:
            self.esem[e] = len(self.semobjs)
            self.semobjs.append(es.enter_context(nc.semaphore("s_" + e)))
        self.cnt = {e: 0 for e in self.esem}
        self.seen = {e: {} for e in self.eng}
        self.dsem = []
        for i in range(NDS):
            self.dsem.append(len(self.semobjs))
            self.semobjs.append(es.enter_context(nc.semaphore("d%d" % i)))
        self.dcnt = [0] * NDS
        self.dnext = 0
        self.dnext_sw = 0
        self.n_inst = 0
        self.n_eng = {e: 0 for e in self.eng}
        self.marks = []

    def _wait(self, e, toks, relax=False):
        need = {}
        for t in toks:
            if t is None:
                continue
            key, val, te = t
            if te == e and e == "pe":
                continue
            if self.seen[e].get(key, 0) >= val:
                continue
            if need.get(key, 0) < val:
                need[key] = val
        for key, val in need.items():
            self.eng[e].wait_ge(self.semobjs[key], val)
            self.seen[e][key] = val

    def _deps(self, reads, writes):
        deps = []
        for b in reads:
            if b.w is not None:
                deps.append(b.w)
        for b in writes:
            if b.w is not None:
                deps.append(b.w)
            for k_, (v_, te_) in b.r.items():
                deps.append((k_, v_, te_))
        return deps

    def _commit(self, tok, reads, writes):
        for b in reads:
            if b.r.get(tok[0], (0, None))[0] < tok[1]:
                b.r[tok[0]] = (tok[1], tok[2])
        for b in writes:
            b.w = tok
            b.r = {}

    def op(self, e, reads, writes, fn, relax=False):
        self._wait(e, self._deps(reads, writes), relax)
        ins = fn()
        self.n_eng[e] += 1
        self.cnt[e] += 1
        ins.then_inc(self.semobjs[self.esem[e]], 1)
        tok = (self.esem[e], self.cnt[e], e)
        self._commit(tok, reads, writes)
        self.n_inst += 1
        return tok

    def grp(self, e, reads, writes, fns):
        self._wait(e, self._deps(reads, writes))
        ins = None
        for fn in fns:
            ins = fn()
            self.n_inst += 1
            self.n_eng[e] += 1
        self.cnt[e] += 1
        ins.then_inc(self.semobjs[self.esem[e]], 1)
        tok = (self.esem[e], self.cnt[e], e)
        self._commit(tok, reads, writes)
        return tok

    def dma(self, q, reads, writes, fn):
        if q == "pool":
            i = self.dnext_sw
            self.dnext_sw = (self.dnext_sw + 1) % NDS_SW
        else:
            i = NDS_SW + self.dnext
            self.dnext = (self.dnext + 1) % (NDS - NDS_SW)
        deps = self._deps(reads, writes)
        if self.dcnt[i] > 0:
            deps.append((self.dsem[i], self.dcnt[i], None))
        self._wait(q, deps)
        ins = fn()
        self.dcnt[i] += 16
        ins.then_inc(self.semobjs[self.dsem[i]], 16)
        tok = (self.dsem[i], self.dcnt[i], None)
        self._commit(tok, reads, writes)
        self.n_inst += 1
        return tok

    def final_wait(self, q, bufs):
        toks = []
        for b in bufs:
            if b.w is not None:
                toks.append(b.w)
        self._wait(q, toks)

    def mark(self, label):
        self.marks.append((label, dict(self.n_eng)))


TT = 256
NCH = TT // 128
SEG = dict(aq=0, ak=768, av=1536, ao=2304, ai=3072, af=3076, az=3080, bq=3848, bk=4232, bv=4616,
           ba=5384, bz=5400, cu=6168, cz=6680, g=7192)
WSHAPES = dict(
    norm_w=[2, 1024], w_in=[2, 1024, INW], mlstm_conv=[2, 4, 1536], mlstm_gate_b=[2, 2, 4],
    mlstm_norm=[2, 768], gla_w_alpha=[2, 16, 384], gla_b_alpha=[2, 384], gla_norm=[2, 768],
    s5_lam_re=[2, 32, 64], s5_lam_im=[2, 32, 64], s5_log_dt=[2, 32], s5_B_re=[2, 32, 64, 16],
    s5_B_im=[2, 32, 64, 16], s5_C_re=[2, 32, 16, 64], s5_C_im=[2, 32, 16, 64], s5_D=[2, 512],
    s5_w_glu=[2, 512, 512], w_branch_mlstm=[2, 768, 1024], w_branch_gla=[2, 768, 1024],
    w_branch_s5=[2, 512, 1024], w_out=[2, 1024, 1024], final_norm=[1024])
EPS = 1e-6
SA = 192 ** -0.5
SB_ = 96 ** -0.5


def host_masks():
    bm = np.zeros((128, 128), np.float32)
    for s_ in range(8):
        for t_ in range(s_, 8):
            bm[16 * s_:16 * (s_ + 1), 16 * t_:16 * (t_ + 1)] = 1.0
    return bm


def host_consts():
    c = np.zeros((128, 3, 128), np.float32)
    c[:, 0, :] = np.eye(128, dtype=np.float32)
    c[:, 1, :] = np.triu(np.ones((128, 128), np.float32))
    c[:, 2, :] = 1.0
    return c


def build(T, NL=2, stage=99, dbgspec=None):
    nc = bass.Bass("TRN2", target_bir_lowering=False)
    NT = T // TT
    dbgspec = dbgspec or {}
    with ExitStack() as es:
        es.enter_context(nc.allow_non_contiguous_dma(reason="small param loads"))
        k = K(nc, es)
        xT_d = nc.dram_tensor("xT", [D, T], F32, kind="ExternalInput").ap()
        cst_d = nc.dram_tensor("consts", [128, 3, 128], F32, kind="ExternalInput").ap()
        bm_d = nc.dram_tensor("bmask", [128, 128], F32, kind="ExternalInput").ap()
        W = {n: nc.dram_tensor(n, s, F32, kind="ExternalInput").ap() for n, s in WSHAPES.items()}
        outT_d = nc.dram_tensor("outT", [D, T], F32, kind="ExternalOutput").ap()
        dbg_d = {n: nc.dram_tensor("dbg_" + n, list(s), F32, kind="ExternalOutput").ap() for n, s in dbgspec.items()}
        dbg_bufs = []

        def sb(name, shape, dt=F32):
            return es.enter_context(nc.sbuf_tensor(name, shape, dt))

        blocks = {}

        def def_block(l, key, parts):
            tot = sum((p.shape[0] // 128) * p.shape[1] for p in parts)
            dt_ = nc.dram_tensor("wb_%d_%s" % (l, key), [128, tot], BF16, kind="Internal").ap()
            b = Buf("wb")
            off = 0
            for p in parts:
                kch, n = p.shape[0] // 128, p.shape[1]
                src = p.rearrange("(k p) n -> p k n", p=128)
                dst = dt_[:, off:off + kch * n].rearrange("p (k n) -> p k n", k=kch)
                k.dma("pool", [], [b], lambda src=src, dst=dst: nc.gpsimd.dma_start(out=dst, in_=src))
                off += kch * n
            blocks[(l, key)] = (dt_, b, tot)

        for l in range(NL):
            wi = W["w_in"][l]
            for nm in ["aq", "ak"]:
                for h in range(2):
                    def_block(l, "%s%d" % (nm, h), [wi[:, SEG[nm] + 384 * h: SEG[nm] + 384 * (h + 1)]])
            def_block(l, "aif", [wi[:, SEG["ai"]:SEG["ai"] + 8]])
            for nm in ["av", "ao", "az"]:
                for h in range(2):
                    def_block(l, "%s%d" % (nm, h), [wi[:, SEG[nm] + 384 * h: SEG[nm] + 384 * (h + 1)]])
            def_block(l, "bq", [wi[:, SEG["bq"]:SEG["bq"] + 384]])
            def_block(l, "bk", [wi[:, SEG["bk"]:SEG["bk"] + 384]])
            def_block(l, "ba", [wi[:, SEG["ba"]:SEG["ba"] + 16]])
            for nm in ["bv", "bz"]:
                for h in range(2):
                    def_block(l, "%s%d" % (nm, h), [wi[:, SEG[nm] + 384 * h: SEG[nm] + 384 * (h + 1)]])
            for h in range(2):
                def_block(l, "cu%d" % h, [wi[:, SEG["cu"] + 256 * h:SEG["cu"] + 256 * (h + 1)]])
                def_block(l, "cz%d" % h, [wi[:, SEG["cz"] + 256 * h:SEG["cz"] + 256 * (h + 1)]])
            def_block(l, "glu", [W["s5_w_glu"][l]])
            for j in range(8):
                def_block(l, "g%d" % j, [wi[:, SEG["g"] + 1024 * i + 128 * j: SEG["g"] + 1024 * i + 128 * (j + 1)] for i in range(3)])
                def_block(l, "br%d" % j, [W["w_branch_mlstm"][l][:, 128 * j:128 * (j + 1)],
                                          W["w_branch_gla"][l][:, 128 * j:128 * (j + 1)],
                                          W["w_branch_s5"][l][:, 128 * j:128 * (j + 1)]])
            for h in range(4):
                def_block(l, "wo%d" % h, [W["w_out"][l][:, 256 * h:256 * (h + 1)]])

        NRING = 3
        ring = [sb("wring%d" % i, [128, 3072], BF16) for i in range(NRING)]
        ring_b = [Buf("ring%d" % i) for i in range(NRING)]
        rstate = [0]

        def wload(l, key):
            dt_, b, tot = blocks[(l, key)]
            i = rstate[0]
            rstate[0] = (i + 1) % NRING
            k.dma("sp", [b] + ([s5t["b"]] if "b" in s5t else []), [ring_b[i]], lambda: nc.sync.dma_start(out=ring[i][:, 0:tot], in_=dt_))
            return ring[i], ring_b[i]

        cst = sb("cst", [128, 3, 128])
        cst_b = Buf("cst")
        k.dma("sp", [], [cst_b], lambda: nc.sync.dma_start(out=cst[:], in_=cst_d))
        cstb = sb("cstb", [128, 3, 128], BF16)
        cstb_b = Buf("cstb")
        k.op("dve", [cst_b], [cstb_b], lambda: nc.vector.tensor_copy(out=cstb[:], in_=cst[:]))
        ident_f, tri_f, ones_f = cst[:, 0, :], cst[:, 1, :], cst[:, 2, :]
        ident_b, tri_b = cstb[:, 0, :], cstb[:, 1, :]

        psf = [es.enter_context(nc.psum_tensor("ps%d" % i, [128, 512], F32)) for i in range(8)]
        ps_b = [Buf("ps%d" % i) for i in range(8)]
        pstate = {"proj": 0, "att": 0, "misc": 0, "pair": 0, "wide": 0}
        PPOOL = {"proj": [0, 1], "att": [3, 4], "misc": [6, 7], "pair": [2, 5], "wide": [0, 1, 3, 4, 2, 5]}
        pmode = {"wide": True}

        def psum(cls):
            if cls == "proj" and pmode["wide"]:
                cls = "wide"
            lst = PPOOL[cls]
            i = lst[pstate[cls] % len(lst)]
            pstate[cls] += 1
            return psf[i], ps_b[i]

        def dbg(name, ap_sb, bufs, view=None):
            if name not in dbg_d:
                return
            db = Buf("dbg")
            dst = dbg_d[name] if view is None else view(dbg_d[name])
            k.dma("pool", bufs, [db], lambda: nc.gpsimd.dma_start(out=dst, in_=ap_sb))
            dbg_bufs.append(db)

        x_sb = sb("x_sb", [128, 8, TT]); x_b = Buf("x")
        JJ = TT // 8
        S5W = 4160
        s5w = sb("s5w", [128, S5W])
        s5t = {}
        Tz_sh = sb("Tz_sh", [128, 32, 128], BF16)
        G_sh = sb("G_sh", [128, 32, 2, 64], BF16)
        H_sh = sb("H_sh", [64, 32, 2, 128], BF16)
        shb = Buf("s5sh"); hsh_b = Buf("s5hsh")
        mg = [sb("mg%d" % i, [128, TT]) for i in range(2)]; mg_b = [Buf("mg") for i in range(2)]
        tzt_b = mg_b
        bmask = sb("bmask_sb", [128, 128])
        k.dma("sp", [], [cst_b], lambda: nc.sync.dma_start(out=bmask[:], in_=bm_d))

        def s5_setup(l, p):
            pb = p["b"]
            V = nc.vector
            if not s5t:
                off = [0]

                def carve(n, parts=64):
                    ap_ = s5w[0:parts, off[0]:off[0] + n]
                    off[0] += n
                    return ap_
                for nm in ["lr", "li", "dtv", "mag", "ang", "t0", "t1", "t2", "cosv", "sinv", "ar", "ai", "nr", "den", "cre", "cim"]:
                    s5t[nm] = carve(32)
                for nm in ["bbre", "bbim", "Ccre", "Ccim"]:
                    s5t[nm] = carve(512).rearrange("p (g h) -> p g h", g=32)
                s5t["Apos"] = carve(576).rearrange("p (n r g) -> p n r g", n=9, r=2)
                s5t["Aneg"] = carve(512).rearrange("p (n r g) -> p n r g", n=8, r=2)
                s5t["Cld"] = s5w[:, off[0]:off[0] + 512].rearrange("p (r c q) -> p r c q", r=2, c=4)
                off[0] += 512
                assert off[0] <= S5W
                r0 = ring[0][0:64, :].bitcast(F32); r1 = ring[1][0:64, :].bitcast(F32); r2 = ring[2][0:64, :].bitcast(F32)
                xs = x_sb[0:64].rearrange("p k t -> p (k t)")
                s5t["Bre"] = r2[:, 0:512].rearrange("p (g h) -> p g h", g=32)
                s5t["Bim"] = r2[:, 512:1024].rearrange("p (g h) -> p g h", g=32)
                s5t["tA"] = xs[:, 0:512].rearrange("p (g h) -> p g h", g=32)
                s5t["tB"] = xs[:, 512:1024].rearrange("p (g h) -> p g h", g=32)
                o2 = [0]
                for nm in ["Lre", "Lim", "GTre", "GTim"]:
                    s5t[nm] = r0[:, o2[0]:o2[0] + 256].rearrange("p (g s h) -> p g s h", g=2, s=8)
                    o2[0] += 256
                s5t["Rre"] = r0[:, o2[0]:o2[0] + 288].rearrange("p (g s h) -> p g s h", g=2, s=9)
                o2 = [0]
                for nm in ["nRim", "u1", "u2"]:
                    s5t[nm] = r1[:, o2[0]:o2[0] + 288].rearrange("p (g s h) -> p g s h", g=2, s=9)
                    o2[0] += 288
                s5t["b"] = Buf("s5tmp")
            tb = s5t["b"]
            T_ = s5t

            def dv(fn):
                k.op("dve", [tb, cst_b], [tb], fn)

            def cmul(ore, oim, are, aim, bre, bim, t1, t2, neg_im=False):
                dv(lambda: V.tensor_tensor(out=t1, in0=are, in1=bre, op=ALU.mult))
                dv(lambda: V.tensor_tensor(out=t2, in0=aim, in1=bim, op=ALU.mult))
                dv(lambda: V.tensor_tensor(out=ore, in0=t1, in1=t2, op=ALU.subtract))
                dv(lambda: V.tensor_tensor(out=t1, in0=are, in1=bim, op=ALU.mult))
                dv(lambda: V.tensor_tensor(out=t2, in0=aim, in1=bre, op=ALU.mult))
                if neg_im:
                    dv(lambda: V.scalar_tensor_tensor(out=oim, in0=t1, scalar=-1.0, in1=t2, op0=ALU.mult, op1=ALU.subtract))
                else:
                    dv(lambda: V.tensor_tensor(out=oim, in0=t1, in1=t2, op=ALU.add))

            k.dma("sp", [], [tb], lambda: nc.sync.dma_start(out=T_["lr"], in_=W["s5_lam_re"][l].rearrange("g p -> p g")))
            k.dma("sp", [], [tb], lambda: nc.sync.dma_start(out=T_["li"], in_=W["s5_lam_im"][l].rearrange("g p -> p g")))
            k.dma("sp", [], [tb], lambda: nc.sync.dma_start(out=T_["dtv"], in_=W["s5_log_dt"][l].partition_broadcast(64)))
            k.dma("sp", [], [tb], lambda: nc.sync.dma_start(out=T_["Bre"], in_=W["s5_B_re"][l].rearrange("g p h -> p g h")))
            k.dma("sp", [], [tb], lambda: nc.sync.dma_start(out=T_["Bim"], in_=W["s5_B_im"][l].rearrange("g p h -> p g h")))
            k.dma("sp", [], [tb], lambda: nc.sync.dma_start(out=T_["Cld"][:, 0], in_=W["s5_C_re"][l].rearrange("g h p -> (g h) p").rearrange("(c q) p -> q c p", q=128)))
            k.dma("sp", [], [tb], lambda: nc.sync.dma_start(out=T_["Cld"][:, 1], in_=W["s5_C_im"][l].rearrange("g h p -> (g h) p").rearrange("(c q) p -> q c p", q=128)))
            p["Drep"] = sb("Drep%d" % l, [128, 32])
            for s_ in range(8):
                k.dma("sp", [], [pb], lambda s_=s_: nc.sync.dma_start(out=p["Drep"][16 * s_:16 * (s_ + 1), :], in_=W["s5_D"][l].rearrange("(g h) -> h g", h=16)))
            k.op("act", [tb], [tb], lambda: nc.scalar.activation(out=T_["dtv"], in_=T_["dtv"], func=AF.Exp))
            dv(lambda: V.tensor_scalar(out=T_["lr"], in0=T_["lr"], scalar1=-1e-4, scalar2=None, op0=ALU.min))
            dv(lambda: V.tensor_tensor(out=T_["t0"], in0=T_["lr"], in1=T_["dtv"], op=ALU.mult))
            k.op("act", [tb], [tb], lambda: nc.scalar.activation(out=T_["mag"], in_=T_["t0"], func=AF.Exp))
            dv(lambda: V.tensor_tensor(out=T_["ang"], in0=T_["li"], in1=T_["dtv"], op=ALU.mult))
            TWO_PI = 2.0 * np.pi
            for dst, shift in [("sinv", 0.0), ("cosv", np.pi / 2)]:
                dv(lambda shift=shift: V.tensor_scalar(out=T_["t0"], in0=T_["ang"], scalar1=float(shift), scalar2=None, op0=ALU.add))
                dv(lambda: V.tensor_copy(out=T_["t2"], in_=T_["t0"]))
                for m in range(10):
                    thr = float((2 * m + 1) * np.pi)
                    dv(lambda thr=thr: V.tensor_scalar(out=T_["t1"], in0=T_["t0"], scalar1=thr, scalar2=TWO_PI, op0=ALU.is_gt, op1=ALU.mult))
                    dv(lambda: V.tensor_tensor(out=T_["t2"], in0=T_["t2"], in1=T_["t1"], op=ALU.subtract))
                dv(lambda: V.tensor_scalar(out=T_["t2"], in0=T_["t2"], scalar1=float(np.pi), scalar2=float(-np.pi), op0=ALU.min, op1=ALU.max))
                k.op("act", [tb], [tb], lambda dst=dst: nc.scalar.activation(out=T_[dst], in_=T_["t2"], func=AF.Sin))
            dv(lambda: V.tensor_tensor(out=T_["ar"], in0=T_["mag"], in1=T_["cosv"], op=ALU.mult))
            dv(lambda: V.tensor_tensor(out=T_["ai"], in0=T_["mag"], in1=T_["sinv"], op=ALU.mult))
            dv(lambda: V.tensor_scalar(out=T_["nr"], in0=T_["ar"], scalar1=-1.0, scalar2=None, op0=ALU.add))
            dv(lambda: V.tensor_tensor(out=T_["t0"], in0=T_["lr"], in1=T_["lr"], op=ALU.mult))
            dv(lambda: V.tensor_tensor(out=T_["t1"], in0=T_["li"], in1=T_["li"], op=ALU.mult))
            dv(lambda: V.tensor_tensor(out=T_["den"], in0=T_["t0"], in1=T_["t1"], op=ALU.add))
            dv(lambda: V.reciprocal(out=T_["den"], in_=T_["den"]))
            dv(lambda: V.tensor_tensor(out=T_["t0"], in0=T_["nr"], in1=T_["lr"], op=ALU.mult))
            dv(lambda: V.tensor_tensor(out=T_["t1"], in0=T_["ai"], in1=T_["li"], op=ALU.mult))
            dv(lambda: V.tensor_tensor(out=T_["t0"], in0=T_["t0"], in1=T_["t1"], op=ALU.add))
            dv(lambda: V.tensor_tensor(out=T_["cre"], in0=T_["t0"], in1=T_["den"], op=ALU.mult))
            dv(lambda: V.tensor_tensor(out=T_["t0"], in0=T_["ai"], in1=T_["lr"], op=ALU.mult))
            dv(lambda: V.tensor_tensor(out=T_["t1"], in0=T_["nr"], in1=T_["li"], op=ALU.mult))
            dv(lambda: V.tensor_tensor(out=T_["t0"], in0=T_["t0"], in1=T_["t1"], op=ALU.subtract))
            dv(lambda: V.tensor_tensor(out=T_["cim"], in0=T_["t0"], in1=T_["den"], op=ALU.mult))
            cre_b = T_["cre"].unsqueeze(2).to_broadcast([64, 32, 16])
            cim_b = T_["cim"].unsqueeze(2).to_broadcast([64, 32, 16])
            cmul(T_["bbre"], T_["bbim"], cre_b, cim_b, T_["Bre"], T_["Bim"], T_["tA"], T_["tB"])
            for ri, nm in enumerate(["Ccre", "Ccim"]):
                for fc in range(4):
                    ps, pbp = psum("misc")
                    k.op("pe", [tb, cst_b], [pbp, tb], lambda ps=ps, ri=ri, fc=fc: nc.tensor.transpose(ps[0:64, 0:128], T_["Cld"][:, ri, fc, :], ident_f))
                    k.op("dve", [pbp], [tb], lambda ps=ps, nm=nm, fc=fc: V.tensor_copy(
                        out=T_[nm][:, 8 * fc:8 * (fc + 1), :], in_=ps[0:64, 0:128].rearrange("p (g h) -> p g h", g=8)))
            Ap, An = T_["Apos"], T_["Aneg"]
            dv(lambda: V.memset(Ap[:, 0, 0, :], 1.0)); dv(lambda: V.memset(Ap[:, 0, 1, :], 0.0))
            dv(lambda: V.memset(An[:, 0, 0, :], 1.0)); dv(lambda: V.memset(An[:, 0, 1, :], 0.0))
            dv(lambda: V.tensor_copy(out=Ap[:, 1, 0, :], in_=T_["ar"])); dv(lambda: V.tensor_copy(out=Ap[:, 1, 1, :], in_=T_["ai"]))
            for n in range(2, 9):
                cmul(Ap[:, n, 0, :], Ap[:, n, 1, :], Ap[:, n - 1, 0, :], Ap[:, n - 1, 1, :], T_["ar"], T_["ai"], T_["t0"], T_["t1"])
            dv(lambda: V.tensor_tensor(out=T_["t0"], in0=T_["ar"], in1=T_["ar"], op=ALU.mult))
            dv(lambda: V.tensor_tensor(out=T_["t1"], in0=T_["ai"], in1=T_["ai"], op=ALU.mult))
            dv(lambda: V.tensor_tensor(out=T_["t0"], in0=T_["t0"], in1=T_["t1"], op=ALU.add))
            dv(lambda: V.reciprocal(out=T_["t0"], in_=T_["t0"]))
            dv(lambda: V.tensor_tensor(out=An[:, 1, 0, :], in0=T_["ar"], in1=T_["t0"], op=ALU.mult))
            dv(lambda: V.scalar_tensor_tensor(out=An[:, 1, 1, :], in0=T_["ai"], scalar=-1.0, in1=T_["t0"], op0=ALU.mult, op1=ALU.mult))
            for n in range(2, 8):
                cmul(An[:, n, 0, :], An[:, n, 1, :], An[:, n - 1, 0, :], An[:, n - 1, 1, :], An[:, 1, 0, :], An[:, 1, 1, :], T_["t0"], T_["t1"])
            p["AR8"] = sb("AR8_%d" % l, [64, 2, 32]); p["AI8"] = sb("AI8_%d" % l, [64, 2, 32])
            k.op("dve", [tb], [pb], lambda: V.tensor_copy(out=p["AR8"][:, 0, :], in_=Ap[:, 8, 0, :]))
            k.op("dve", [tb], [pb], lambda: V.tensor_copy(out=p["AR8"][:, 1, :], in_=Ap[:, 8, 0, :]))
            k.op("dve", [tb], [pb], lambda: V.tensor_scalar(out=p["AI8"][:, 0, :], in0=Ap[:, 8, 1, :], scalar1=-1.0, scalar2=None, op0=ALU.mult))
            k.op("dve", [tb], [pb], lambda: V.tensor_copy(out=p["AI8"][:, 1, :], in_=Ap[:, 8, 1, :]))
            p["Xc"] = sb("Xc%d" % l, [64, 2, 32]); p["Xc_b"] = Buf("Xc")
            k.op("pool", [], [p["Xc_b"]], lambda: nc.gpsimd.memset(p["Xc"][:], 0.0))
            dbg("s5ar", T_["ar"], [tb])
            dbg("s5ai", T_["ai"], [tb])
            p["Tz_d"] = nc.dram_tensor("Tz_d%d" % l, [128, 32 * 128], BF16, kind="Internal").ap()
            p["G_d"] = nc.dram_tensor("G_d%d" % l, [128, 32 * 2 * 64], BF16, kind="Internal").ap()
            p["H_d"] = nc.dram_tensor("H_d%d" % l, [64, 32 * 2 * 128], BF16, kind="Internal").ap()
            p["s5d_b"] = Buf("s5d")
            p["s5h_b"] = [Buf("s5h0"), Buf("s5h1")]
            for blk in range(16):
                g0 = 2 * blk
                sl = slice(g0, g0 + 2)
                u1 = T_["u1"]; u2 = T_["u2"]
                a_re = An[:, :, 0, sl].rearrange("p s g -> p g s").unsqueeze(3).to_broadcast([64, 2, 8, 16])
                a_im = An[:, :, 1, sl].rearrange("p s g -> p g s").unsqueeze(3).to_broadcast([64, 2, 8, 16])
                b_re = T_["bbre"][:, sl, :].unsqueeze(2).to_broadcast([64, 2, 8, 16])
                b_im = T_["bbim"][:, sl, :].unsqueeze(2).to_broadcast([64, 2, 8, 16])
                cmul(T_["Lre"], T_["Lim"], a_re, a_im, b_re, b_im, u1[:, :, 0:8, :], u2[:, :, 0:8, :])
                a_re = Ap[:, :, 0, sl].rearrange("p s g -> p g s").unsqueeze(3).to_broadcast([64, 2, 9, 16])
                a_im = Ap[:, :, 1, sl].rearrange("p s g -> p g s").unsqueeze(3).to_broadcast([64, 2, 9, 16])
                c_re = T_["Ccre"][:, sl, :].unsqueeze(2).to_broadcast([64, 2, 9, 16])
                c_im = T_["Ccim"][:, sl, :].unsqueeze(2).to_broadcast([64, 2, 9, 16])
                cmul(T_["Rre"], T_["nRim"], a_re, a_im, c_re, c_im, u1, u2, neg_im=True)
                a_re = Ap[:, 7, 0, sl].unsqueeze(2).unsqueeze(3).to_broadcast([64, 2, 8, 16])
                a_im = Ap[:, 7, 1, sl].unsqueeze(2).unsqueeze(3).to_broadcast([64, 2, 8, 16])
                cmul(T_["GTre"], T_["GTim"], a_re, a_im, T_["Lre"], T_["Lim"], u1[:, :, 0:8, :], u2[:, :, 0:8, :])
                for gi in range(2):
                    g = g0 + gi
                    ps, pbp = psum("att")
                    k.grp("pe", [tb], [pbp, tb], [
                        lambda ps=ps, gi=gi: nc.tensor.matmul(ps[:, 0:128], lhsT=T_["Lre"][:, gi].rearrange("p s h -> p (s h)"), rhs=T_["Rre"][:, gi, 0:8, :].rearrange("p s h -> p (s h)"), start=True, stop=False),
                        lambda ps=ps, gi=gi: nc.tensor.matmul(ps[:, 0:128], lhsT=T_["Lim"][:, gi].rearrange("p s h -> p (s h)"), rhs=T_["nRim"][:, gi, 0:8, :].rearrange("p s h -> p (s h)"), start=False, stop=True)])
                    ci = g % 2
                    k.op("dve", [pbp, cst_b], [tzt_b[ci]], lambda ps=ps, ci=ci: V.tensor_tensor(out=mg[ci][:, 0:128], in0=ps[:, 0:128], in1=bmask[:], op=ALU.mult))
                    k.op("dve", [tzt_b[ci], cst_b, pb], [shb], lambda ci=ci, g=g: V.scalar_tensor_tensor(
                        out=Tz_sh[:, g, :], in0=ident_f, scalar=p["Drep"][:, g:g + 1], in1=mg[ci][:, 0:128], op0=ALU.mult, op1=ALU.add))
                ps, pbp = psum("misc")
                k.grp("pe", [tb, cst_b], [pbp, tb], [
                    (lambda ps=ps, gi=gi, ri=ri: nc.tensor.transpose(ps[:, 64 * (2 * gi + ri):64 * (2 * gi + ri + 1)],
                                                                    T_["GTre" if ri == 0 else "GTim"][:, gi].rearrange("p s h -> p (s h)"), ident_f[0:64, 0:64]))
                    for gi in range(2) for ri in range(2)])
                k.op("dve", [pbp], [shb], lambda ps=ps, g0=g0: V.tensor_copy(out=G_sh[:, g0:g0 + 2, :, :].rearrange("p g r q -> p (g r q)"), in_=ps[:, 0:256]))
                gl = g0
                hb_ = p["s5h_b"][0]
                k.op("dve", [tb, hb_], [hsh_b], lambda gl=gl: V.tensor_copy(out=H_sh[:, gl:gl + 2, 0, :], in_=T_["Rre"][:, :, 1:9, :].rearrange("p g s h -> p g (s h)")))
                k.op("dve", [tb, hb_], [hsh_b], lambda gl=gl: V.tensor_copy(out=H_sh[:, gl:gl + 2, 1, :], in_=T_["nRim"][:, :, 1:9, :].rearrange("p g s h -> p g (s h)")))
            k.dma("sp", [hsh_b], [p["s5h_b"][0]], lambda: nc.sync.dma_start(out=p["H_d"], in_=H_sh[:].rearrange("p g r q -> p (g r q)")))
            k.dma("sp", [shb], [p["s5d_b"]], lambda: nc.sync.dma_start(out=p["Tz_d"], in_=Tz_sh[:].rearrange("p g q -> p (g q)")))
            k.dma("sp", [shb], [p["s5d_b"]], lambda: nc.sync.dma_start(out=p["G_d"], in_=G_sh[:].rearrange("p g r q -> p (g r q)")))

        P = []
        for l in range(NL):
            p = {}
            pb = Buf("params%d" % l)
            p["b"] = pb
            p["nw"] = sb("nw%d" % l, [128, 8])
            k.dma("sp", [], [pb], lambda l=l, p=p: nc.sync.dma_start(out=p["nw"][:], in_=W["norm_w"][l].rearrange("(k p) -> p k", p=128)))
            p["cw"] = sb("cw%d" % l, [96, 16, 4])
            for j in range(4):
                k.dma("sp", [], [pb], lambda l=l, p=p, j=j: nc.sync.dma_start(out=p["cw"][:, :, j], in_=W["mlstm_conv"][l][j].rearrange("(b p) -> p b", p=96)))
            p["gbias"] = sb("gbias%d" % l, [128, NCH, 8])
            for c in range(NCH):
                k.dma("sp", [], [pb], lambda l=l, p=p, c=c: nc.sync.dma_start(out=p["gbias"][:, c, :], in_=W["mlstm_gate_b"][l].rearrange("a b -> (a b)").partition_broadcast(128)))
            if l == 0:
                nw_sh = (sb("nwa_sh", [128, 768]), sb("nwb_sh", [128, 768]), Buf("nwsh"))
            p["nwa"], p["nwb"], p["nwsh_b"] = nw_sh
            p["C32"] = sb("C32_%d" % l, [96, 8, 193])
            p["Cbf"] = sb("Cbf_%d" % l, [96, 8, 193], BF16)
            p["C_b"] = [Buf("C%d_%d" % (l, i)) for i in range(8)]
            for i in range(8):
                k.op("pool", [], [p["C_b"][i]], lambda p=p, i=i: nc.gpsimd.memset(p["C32"][:, i, :], 0.0))
                k.op("pool", [], [p["C_b"][i]], lambda p=p, i=i: nc.gpsimd.memset(p["Cbf"][:, i, :], 0.0))
            p["wal"] = sb("wal%d" % l, [16, 384], BF16)
            k.dma("pool", [], [pb], lambda l=l, p=p: nc.gpsimd.dma_start(out=p["wal"][:], in_=W["gla_w_alpha"][l]))
            p["bal"] = sb("bal%d" % l, [1, 384], BF16)
            k.dma("pool", [], [pb], lambda l=l, p=p: nc.gpsimd.dma_start(out=p["bal"][:], in_=W["gla_b_alpha"][l].rearrange("(a n) -> a n", a=1)))
            p["S32"] = sb("S32_%d" % l, [96, 4, 192])
            p["Sbf"] = sb("Sbf_%d" % l, [96, 4, 192], BF16)
            p["S_b"] = [Buf("S%d_%d" % (l, i)) for i in range(4)]
            for i in range(4):
                k.op("pool", [], [p["S_b"][i]], lambda p=p, i=i: nc.gpsimd.memset(p["S32"][:, i, :], 0.0))
                k.op("pool", [], [p["S_b"][i]], lambda p=p, i=i: nc.gpsimd.memset(p["Sbf"][:, i, :], 0.0))
            if stage >= 3:
                s5_setup(l, p)
            p["halo"] = sb("halo%d" % l, [96, 16, 3])
            p["halo_b"] = [Buf("halo") for i in range(16)]
            for i in range(16):
                k.op("pool", [], [p["halo_b"][i]], lambda p=p, i=i: nc.gpsimd.memset(p["halo"][:, i, :], 0.0))
            P.append(p)

        hn = sb("hn", [128, 8, TT], BF16); hn_b = Buf("hn")
        sq = [sb("sq%d" % i, [128, TT]) for i in range(1)]; sq_b = [Buf("sq") for i in range(1)]
        rstd = sb("rstd", [128, TT]); rstd_b = Buf("rstd")
        qT = sb("qT", [96, 8, TT], BF16); qT_b = [Buf("qT%d" % i) for i in range(8)]
        kT = sb("kT", [96, 8, TT], BF16); kT_b = [Buf("kT%d" % i) for i in range(8)]
        cs = [sb("cs%d" % i, [96, TT + 3]) for i in range(2)]; cs_b = [Buf("cs") for i in range(2)]; csh_b = [Buf("csh") for i in range(2)]
        acc = [sb("acc%d" % i, [96, TT]) for i in range(2)]; acc_b = [Buf("acc") for i in range(2)]
        g_sb = sb("g_sb", [128, NCH, 8]); g_b = Buf("g")
        lf = sb("lf", [128, NCH, 4]); lf_b = Buf("lf")
        a_sb = sb("a_sb", [128, NCH * 4]); e_sb = sb("e_sb", [128, NCH * 4]); ir_sb = sb("ir_sb", [128, NCH * 4]); dec_sb = sb("dec_sb", [128, NCH * 4])
        a_b, e_b, r_b, dec_b = Buf("a"), Buf("e"), Buf("r"), Buf("dec")
        v2 = sb("v2", [128, NCH, 4, 193], BF16); v2_b = [Buf("v2_%d" % c) for c in range(NCH)]
        vtmp = sb("vtmp", [128, 4, 193]); vtmp_b = Buf("vtmp")
        k.op("pool", [], [vtmp_b], lambda: nc.gpsimd.memset(vtmp[:, :, 192:193], SA))
        ktok = sb("ktok", [128, NCH, 768], BF16); ktok_b = [Buf("ktok%d" % c) for c in range(NCH)]
        gsa = sb("gsa", [128, NCH, 768], BF16); gsa_b = [Buf("gsa%d" % c) for c in range(NCH)]
        gsb = gsa; gsb_b = gsa_b
        gt = [sb("gt%d" % i, [128, 384], BF16) for i in range(2)]; gt_b = [Buf("gt") for i in range(2)]
        stm = [sb("stm%d" % i, [128, 4, 128], BF16) for i in range(2)]; stm_b = [Buf("stm") for i in range(2)]
        sm = [sb("sm%d" % i, [128, 48]) for i in range(2)]; sm_b = [Buf("sm") for i in range(2)]
        ctmp = [sb("ctmp%d" % i, [96, 386]) for i in range(2)]; ctmp_b = [Buf("ctmp") for i in range(2)]
        ytmp = [sb("ytmp%d" % i, [128, 192]) for i in range(2)]; ytmp_b = [Buf("ytmp") for i in range(2)]
        ya = [sb("ya%d" % i, [128, 768], BF16) for i in range(2)]; ya_b = [Buf("ya") for i in range(2)]
        yaT = sb("yaT", [128, 6, TT], BF16); yaT_b = Buf("yaT")
        baT = sb("baT", [16, TT], BF16); baT_b = Buf("baT")
        la = [sb("la%d" % i, [128, 384]) for i in range(NCH)]; la_b = [Buf("la") for i in range(NCH)]
        eq = sb("eq", [96, 4, TT]); ek = sb("ek", [96, 4, TT]); eqk_b = [Buf("eqk%d" % h) for h in range(4)]
        dgl = sb("dgl", [96, 4, NCH]); dgl_b = [Buf("dgl%d" % h) for h in range(4)]
        qg = qT; qg_b = qT_b[0:4]
        kg = kT; kg_b = kT_b[0:4]
        vg = sb("vg", [128, NCH, 768], BF16); vg_b = [Buf("vg%d" % c) for c in range(NCH)]
        ktokg = sb("ktokg", [128, NCH, 384], BF16); ktokg_b = [Buf("ktokg%d" % c) for c in range(NCH)]
        ybT = sb("ybT", [128, 6, TT], BF16); ybT_b = Buf("ybT")
        Ut = s5w[0:32, 0:2048].bitcast(BF16).rearrange("p (g s h) -> p g s h", g=32, s=8); Ut_b = Buf("Ut")
        Yt = s5w[0:32, 0:2048].bitcast(BF16).rearrange("p (t g h) -> p t g h", t=8, g=32); Yt_b = Ut_b
        Xall = s5w[0:64, 2048:2048 + (JJ + 1) * 64].rearrange("p (j r g) -> p j r g", r=2, g=32); Xall_b = Buf("Xall")
        U_all = sb("U_all", [128, 32, JJ], BF16); U_b = Buf("U_all")
        xbf = sb("xbf", [64, 2, 32, JJ], BF16); xbf_b = Buf("xbf")
        st1 = sb("st1", [64, 2, 32]); st2 = sb("st2", [64, 2, 32]); st_b = Buf("st"); st1_b = Buf("st1"); st2a_b = Buf("st2a"); st2b_b = Buf("st2b")
        yc0 = sb("yc0", [128, 4, TT]); yc0b = sb("yc0b", [128, 4, TT], BF16); yc0_b = [Buf("yc0_%d" % i) for i in range(4)]
        czs = sb("czs", [128, 4, TT], BF16); czs_b = [Buf("czs%d" % i) for i in range(4)]
        ycT = sb("ycT", [128, 4, TT], BF16); ycT_b = [Buf("ycT%d" % i) for i in range(4)]
        sg = [sb("sg%d" % i, [128, 3, TT], BF16) for i in range(2)]; sg_b = [Buf("sg") for i in range(2)]
        mT = sb("mT", [128, 8, TT], BF16); mT_b = [Buf("mT%d" % i) for i in range(8)]
        fnw = sb("fnw", [128, 8]); fnw_b = Buf("fnw")
        k.dma("sp", [], [fnw_b], lambda: nc.sync.dma_start(out=fnw[:], in_=W["final_norm"].rearrange("(k p) -> p k", p=128)))
        o_sb = [sb("o_sb%d" % i, [128, TT]) for i in range(1)]; o_b = [Buf("o") for i in range(1)]
        rr = {"sg": 0, "mg": 0, "sq": 0, "cs": 0, "gt": 0, "stm": 0, "sm": 0, "ctmp": 0, "ytmp": 0, "ya": 0}

        def rot(name, n=2):
            i = rr[name] % n
            rr[name] += 1
            return i

        def proj_fm(wt, wb, ncols, col_off, M, cls="proj"):
            ps, pb = psum(cls)
            wv = wt[:, 0:8 * ncols].rearrange("p (k n) -> p k n", k=8)
            k.grp("pe", [wb, hn_b], [pb], [
                (lambda kk=kk: nc.tensor.matmul(ps[0:M, 0:TT], lhsT=wv[:, kk, col_off:col_off + M], rhs=hn[:, kk, :], start=(kk == 0), stop=(kk == 7)))
                for kk in range(8)])
            return ps, pb

        def proj_tm(wt, wb, ncols, col_off, N, c, cls="proj", ps_pb=None, pcol=0):
            ps, pb = ps_pb if ps_pb is not None else psum(cls)
            wv = wt[:, 0:8 * ncols].rearrange("p (k n) -> p k n", k=8)
            k.grp("pe", [wb, hn_b], [pb], [
                (lambda kk=kk: nc.tensor.matmul(ps[:, pcol:pcol + N], lhsT=hn[:, kk, c * 128:(c + 1) * 128], rhs=wv[:, kk, col_off:col_off + N], start=(kk == 0), stop=(kk == 7)))
                for kk in range(8)])
            return ps, pb

        def norm_stage(l, nw_ap, pbuf):
            ps, pb = psum("misc")
            for kk in range(8):
                i = 0
                k.op("act", [x_b], [sq_b[i]], lambda kk=kk, i=i: nc.scalar.activation(out=sq[i][:], in_=x_sb[:, kk, :], func=AF.Square))
                k.op("pe", [sq_b[i], cst_b], [pb], lambda kk=kk, i=i: nc.tensor.matmul(ps[:, 0:TT], lhsT=ones_f, rhs=sq[i][:], start=(kk == 0), stop=(kk == 7)))
            k.op("act", [pb], [rstd_b], lambda: nc.scalar.activation(out=rstd[:], in_=ps[:, 0:TT], func=AF.Sqrt, scale=1.0 / D, bias=EPS))
            k.op("dve", [rstd_b], [rstd_b], lambda: nc.vector.reciprocal(out=rstd[:], in_=rstd[:]))

        def chain2(g1, g2):
            for _ in g1:
                yield
            for _ in g2:
                yield

        def interleave(ga, gb, ratio):
            da = db = False
            while not (da and db):
                if not da:
                    try:
                        next(ga)
                    except StopIteration:
                        da = True
                for _ in range(ratio):
                    if not db:
                        try:
                            next(gb)
                        except StopIteration:
                            db = True

        def layer(l, t):
            p = P[l]
            k.mark("L%d_t%d_start" % (l, t))
            k.dma("sp", [], [p["nwsh_b"]], lambda: nc.sync.dma_start(out=p["nwa"][:], in_=W["mlstm_norm"][l].partition_broadcast(128)))
            k.dma("sp", [], [p["nwsh_b"]], lambda: nc.sync.dma_start(out=p["nwb"][:], in_=W["gla_norm"][l].partition_broadcast(128)))
            if stage >= 3:
                k.dma("sp", [p["s5d_b"]], [shb], lambda: nc.sync.dma_start(out=Tz_sh[:].rearrange("p g q -> p (g q)"), in_=p["Tz_d"]))
                k.dma("sp", [p["s5d_b"]], [shb], lambda: nc.sync.dma_start(out=G_sh[:].rearrange("p g r q -> p (g r q)"), in_=p["G_d"]))
                k.dma("sp", [p["s5h_b"][0]], [hsh_b], lambda: nc.sync.dma_start(out=H_sh[:].rearrange("p g r q -> p (g r q)"), in_=p["H_d"]))
            norm_stage(l, p["nw"], p["b"])
            for kk in range(8):
                k.op("dve", [x_b, rstd_b, p["b"]], [hn_b], lambda kk=kk: nc.vector.scalar_tensor_tensor(
                    out=hn[:, kk, :], in0=x_sb[:, kk, :], scalar=p["nw"][:, kk:kk + 1], in1=rstd[:], op0=ALU.mult, op1=ALU.mult), relax=True)
            if stage < 1:
                return
            for qi, (nm, dstT, dst_b) in enumerate([("aq", qT, qT_b), ("ak", kT, kT_b)]):
                for h2 in range(2):
                    wt, wb = wload(l, "%s%d" % (nm, h2))
                    def _cstream(js, ci, qi=qi, h2=h2, wt=wt, wb=wb, dstT=dstT, dst_b=dst_b):
                        for j in js:
                            blk = 4 * h2 + j
                            gb = 8 * qi + blk
                            ps, pb = proj_fm(wt, wb, 384, 96 * j, 96)
                            hb = p["halo_b"][gb]
                            k.op("act", [hb], [csh_b[ci]], lambda ci=ci, gb=gb: nc.scalar.copy(out=cs[ci][:, 0:3], in_=p["halo"][:, gb, :]))
                            k.op("act", [pb], [cs_b[ci]], lambda ci=ci, ps=ps: nc.scalar.copy(out=cs[ci][:, 3:TT + 3], in_=ps[0:96, 0:TT]))
                            k.op("act", [pb], [hb], lambda ps=ps, gb=gb: nc.scalar.copy(out=p["halo"][:, gb, :], in_=ps[0:96, TT - 3:TT]))
                            yield
                            k.op("dve", [cs_b[ci], csh_b[ci], p["b"]], [acc_b[ci]], lambda ci=ci, gb=gb: nc.vector.tensor_scalar(
                                out=acc[ci][:], in0=cs[ci][:, 0:TT], scalar1=p["cw"][:, gb, 0:1], scalar2=None, op0=ALU.mult))
                            yield
                            for jj in range(1, 4):
                                k.op("dve", [cs_b[ci], csh_b[ci], p["b"], acc_b[ci]], [acc_b[ci]], lambda ci=ci, gb=gb, jj=jj: nc.vector.scalar_tensor_tensor(
                                    out=acc[ci][:], in0=cs[ci][:, jj:jj + TT], scalar=p["cw"][:, gb, jj:jj + 1], in1=acc[ci][:], op0=ALU.mult, op1=ALU.add))
                                yield
                            k.op("act", [acc_b[ci]], [dst_b[blk]], lambda ci=ci, blk=blk, dstT=dstT: nc.scalar.activation(out=dstT[:, blk, :], in_=acc[ci][:], func=AF.Silu))
                    interleave(_cstream([0, 2], 0), _cstream([1, 3], 1), 1)
            dbg("qT", qT[:], qT_b)
            dbg("kT", kT[:], kT_b)
            k.mark("L%d_t%d_qkdone" % (l, t))
            wt, wb = wload(l, "aif")
            psg, pbg = psum("misc")
            for c in range(NCH):
                proj_tm(wt, wb, 8, 0, 8, c, ps_pb=(psg, pbg), pcol=8 * c)
            k.op("dve", [pbg, p["b"]], [g_b], lambda: nc.vector.tensor_tensor(
                out=g_sb[:].rearrange("p c e -> p (c e)"), in0=psg[:, 0:8 * NCH], in1=p["gbias"][:].rearrange("p c e -> p (c e)"), op=ALU.add))
            k.op("act", [g_b], [lf_b], lambda: nc.scalar.activation(out=lf[:], in_=g_sb[:, :, 4:8], func=AF.Sigmoid))
            k.op("act", [lf_b], [lf_b], lambda: nc.scalar.activation(out=lf[:], in_=lf[:], func=AF.Ln))
            lf2 = lf[:].rearrange("p c e -> p (c e)")
            for h2 in range(2):
                wto, wbo = wload(l, "ao%d" % h2)
                wtz, wbz = wload(l, "az%d" % h2)
                for c in range(NCH):
                    ps, pb = proj_tm(wto, wbo, 384, 0, 384, c)
                    i0 = rot("gt")
                    k.op("act", [pb], [gt_b[i0]], lambda ps=ps, i0=i0: nc.scalar.activation(out=gt[i0][:], in_=ps[:, 0:384], func=AF.Sigmoid))
                    ps, pb = proj_tm(wtz, wbz, 384, 0, 384, c)
                    i1 = rot("gt")
                    k.op("act", [pb], [gt_b[i1]], lambda ps=ps, i1=i1: nc.scalar.activation(out=gt[i1][:], in_=ps[:, 0:384], func=AF.Silu))
                    k.op("pool", [gt_b[i0], gt_b[i1]], [gsa_b[c]], lambda c=c, h2=h2, i0=i0, i1=i1: nc.gpsimd.tensor_tensor(
                        out=gsa[:, c, 384 * h2:384 * (h2 + 1)], in0=gt[i0][:], in1=gt[i1][:], op=ALU.mult))
            psb, pbb = psum("misc")
            k.op("pe", [lf_b, cst_b], [pbb], lambda: nc.tensor.matmul(psb[:, 0:4 * NCH], lhsT=tri_f, rhs=lf2, start=True, stop=True))
            k.op("pe", [lf_b, cst_b], [pbb], lambda: nc.tensor.matmul(psb[:, 64:64 + 4 * NCH], lhsT=ones_f, rhs=lf2, start=True, stop=True))
            k.op("dve", [g_b, pbb], [a_b], lambda: nc.vector.tensor_tensor(
                out=a_sb[:].rearrange("p (c e) -> p c e", c=NCH), in0=g_sb[:, :, 0:4], in1=psb[:, 0:4 * NCH].rearrange("p (c e) -> p c e", c=NCH), op=ALU.subtract))
            k.op("act", [a_b], [e_b], lambda: nc.scalar.activation(out=e_sb[:], in_=a_sb[:], func=AF.Exp))
            k.op("act", [pbb], [r_b], lambda: nc.scalar.activation(out=ir_sb[:], in_=psb[:, 0:4 * NCH], func=AF.Exp, scale=-1.0))
            k.op("act", [pbb], [dec_b], lambda: nc.scalar.activation(out=dec_sb[:], in_=psb[:, 64:64 + 4 * NCH], func=AF.Exp))
            for h2 in range(2):
                wt, wb = wload(l, "av%d" % h2)
                for c in range(NCH):
                    ps, pb = proj_tm(wt, wb, 384, 0, 384, c)
                    k.op("act", [pb], [vtmp_b], lambda ps=ps, h2=h2: nc.scalar.activation(
                        out=vtmp[:, 2 * h2:2 * h2 + 2, 0:192], in_=ps[:, 0:384].rearrange("p (h e) -> p h e", h=2), func=AF.Copy, scale=SA))
                    for hh in range(2):
                        h = 2 * h2 + hh
                        k.op("dve", [vtmp_b, e_b], [v2_b[c]], lambda c=c, h=h: nc.vector.tensor_scalar(
                            out=v2[:, c, h, :], in0=vtmp[:, h, :], scalar1=e_sb[:, 4 * c + h:4 * c + h + 1], scalar2=None, op0=ALU.mult))
            for c in range(NCH):
                ps, pb = psum("misc")
                psv = ps[:].bitcast(BF16)
                k.grp("pe", kT_b + [cstb_b], [pb], [
                    (lambda blk=blk: nc.tensor.transpose(psv[:, 96 * blk:96 * (blk + 1)], kT[:, blk, c * 128:(c + 1) * 128], ident_b[0:96, 0:96]))
                    for blk in range(8)])
                k.op("act", [pb], [ktok_b[c]], lambda c=c, psv=psv: nc.scalar.copy(out=ktok[:, c, :], in_=psv[:, 0:768]))
            k.mark("L%d_t%d_mloop" % (l, t))
            def _gla_pre():
                k.mark("L%d_t%d_gla" % (l, t))
                wt, wb = wload(l, "ba")
                ps, pb = proj_fm(wt, wb, 16, 0, 16)
                k.op("act", [pb], [baT_b], lambda ps=ps: nc.scalar.copy(out=baT[:], in_=ps[0:16, 0:TT]))
                yield
                for c in range(NCH):
                    ps, pb = psum("proj")
                    k.grp("pe", [baT_b, p["b"], cstb_b], [pb], [
                        lambda ps=ps, c=c: nc.tensor.matmul(ps[:, 0:384], lhsT=baT[:, c * 128:(c + 1) * 128], rhs=p["wal"][:], start=True, stop=False),
                        lambda ps=ps: nc.tensor.matmul(ps[:, 0:384], lhsT=cstb[0:1, 2, :], rhs=p["bal"][:], start=False, stop=True)])
                    yield
                    k.op("act", [pb], [la_b[c]], lambda ps=ps, c=c: nc.scalar.activation(out=la[c][:], in_=ps[:, 0:384], func=AF.Sigmoid))
                    yield
                    k.op("act", [la_b[c]], [la_b[c]], lambda c=c: nc.scalar.activation(out=la[c][:], in_=la[c][:], func=AF.Ln))
                    yield
                for h in range(4):
                    ps, pb = psum("proj")
                    for c in range(NCH):
                        k.op("pe", [la_b[c], cst_b], [pb], lambda ps=ps, c=c, h=h: nc.tensor.matmul(
                            ps[0:96, c * 128:(c + 1) * 128], lhsT=la[c][:, 96 * h:96 * (h + 1)], rhs=tri_f, start=True, stop=True))
                        yield
                    k.op("act", [pb], [eqk_b[h]], lambda ps=ps, h=h: nc.scalar.activation(out=eq[:, h, :], in_=ps[0:96, 0:TT], func=AF.Exp, scale=1.0 / 16))
                    yield
                    k.op("act", [pb], [eqk_b[h]], lambda ps=ps, h=h: nc.scalar.activation(out=ek[:, h, :], in_=ps[0:96, 0:TT], func=AF.Exp, scale=-1.0 / 16))
                    yield
                    k.op("pool", [eqk_b[h]], [dgl_b[h]], lambda h=h: nc.gpsimd.tensor_copy(
                        out=dgl[:, h, :], in_=eq[:, h, :].rearrange("p (c t) -> p c t", c=NCH)[:, :, 127]))
                    yield
                for nm, dst, dst_b, ee, scl in [("bq", qg, qg_b, eq, SB_), ("bk", kg, kg_b, ek, 1.0)]:
                    wt, wb = wload(l, nm)
                    for h in range(4):
                        ps, pb = proj_fm(wt, wb, 384, 96 * h, 96)
                        k.op("dve", [pb, eqk_b[h]], [dst_b[h]], lambda ps=ps, h=h, dst=dst, ee=ee, scl=scl: nc.vector.scalar_tensor_tensor(
                            out=dst[:, h, :], in0=ps[0:96, 0:TT], scalar=scl, in1=ee[:, h, :], op0=ALU.mult, op1=ALU.mult))
                        yield
                for h2 in range(2):
                    wt, wb = wload(l, "bv%d" % h2)
                    for c in range(NCH):
                        ps, pb = proj_tm(wt, wb, 384, 0, 384, c)
                        k.op("act", [pb], [vg_b[c]], lambda ps=ps, c=c, h2=h2: nc.scalar.copy(out=vg[:, c, 384 * h2:384 * (h2 + 1)], in_=ps[:, 0:384]))
                        yield
                for h2 in range(2):
                    wt, wb = wload(l, "bz%d" % h2)
                    for c in range(NCH):
                        ps, pb = proj_tm(wt, wb, 384, 0, 384, c)
                        k.op("act", [pb], [gsb_b[c]], lambda ps=ps, c=c, h2=h2: nc.scalar.activation(out=gsb[:, c, 384 * h2:384 * (h2 + 1)], in_=ps[:, 0:384], func=AF.Silu))
                        yield
                for c in range(NCH):
                    ps, pb = psum("proj")
                    psv = ps[:].bitcast(BF16)
                    k.grp("pe", kg_b + [cstb_b], [pb], [
                        (lambda h=h, c=c, psv=psv: nc.tensor.transpose(psv[:, 96 * h:96 * (h + 1)], kg[:, h, c * 128:(c + 1) * 128], ident_b[0:96, 0:96]))
                        for h in range(4)])
                    yield
                    k.op("act", [pb], [ktokg_b[c]], lambda c=c, psv=psv: nc.scalar.copy(out=ktokg[:, c, :], in_=psv[:, 0:384]))
                    yield

            def _mloop():
                V = nc.vector
                tri4 = tri_f.unsqueeze(1).to_broadcast([128, 4, 128])
                for c in range(NCH):
                    tsl = slice(c * 128, (c + 1) * 128)
                    yi = rot("ya")
                    pss, pbs = psum("att")
                    k.grp("pe", kT_b + qT_b, [pbs], [
                        (lambda j=j, h=h, pss=pss: nc.tensor.matmul(pss[:, 128 * h:128 * (h + 1)], lhsT=kT[:, 2 * h + j, tsl], rhs=qT[:, 2 * h + j, tsl], start=(j == 0), stop=(j == 1)))
                        for h in range(4) for j in range(2)])
                    si = rot("stm")
                    k.op("dve", [pbs, cst_b], [stm_b[si]], lambda si=si, pss=pss: V.tensor_tensor(
                        out=stm[si][:], in0=pss[:, 0:512].rearrange("p (h t) -> p h t", h=4), in1=tri4, op=ALU.mult))
                    yield
                    pair = []
                    for hp in range(2):
                        pso, pbo = psum("pair")
                        fns = []
                        for hh in range(2):
                            h = 2 * hp + hh
                            oc = 256 * hh
                            fns.append(lambda si=si, pso=pso, h=h, oc=oc: nc.tensor.matmul(pso[:, oc:oc + 193], lhsT=stm[si][:, h, :], rhs=v2[:, c, h, :], start=True, stop=False))
                            for j in range(2):
                                fns.append(lambda j=j, pso=pso, h=h, oc=oc: nc.tensor.matmul(pso[:, oc:oc + 193], lhsT=qT[:, 2 * h + j, tsl], rhs=p["Cbf"][:, 2 * h + j, :], start=False, stop=(j == 1)))
                        k.grp("pe", [stm_b[si], v2_b[c]] + qT_b[4 * hp:4 * hp + 4] + p["C_b"][4 * hp:4 * hp + 4], [pbo], fns)
                        pair.append((pso, pbo))
                    yield
                    for h in range(4):
                        psd, pbd = psum("att")
                        k.grp("pe", [ktok_b[c], v2_b[c]], [pbd], [
                            (lambda j=j, psd=psd, h=h: nc.tensor.matmul(psd[0:96, 193 * j:193 * (j + 1)], lhsT=ktok[:, c, 96 * (2 * h + j):96 * (2 * h + j + 1)], rhs=v2[:, c, h, :], start=True, stop=True))
                            for j in range(2)])
                        ti = rot("ctmp")
                        cbs = [p["C_b"][2 * h], p["C_b"][2 * h + 1]]
                        dcol = dec_sb[0:96, 4 * c + h:4 * c + h + 1]
                        c32v = p["C32"][:, 2 * h:2 * h + 2, :].rearrange("p a e -> p (a e)")
                        cbfv = p["Cbf"][:, 2 * h:2 * h + 2, :].rearrange("p a e -> p (a e)")
                        k.op("dve", [pbd] + cbs, [ctmp_b[ti]], lambda psd=psd, ti=ti, c32v=c32v: V.tensor_tensor(out=ctmp[ti][:], in0=psd[0:96, 0:386], in1=c32v, op=ALU.add))
                        k.op("dve", [ctmp_b[ti], dec_b], cbs, lambda ti=ti, c32v=c32v, dcol=dcol: V.tensor_scalar(out=c32v, in0=ctmp[ti][:], scalar1=dcol, scalar2=None, op0=ALU.mult))
                        k.op("act", [ctmp_b[ti], dec_b], cbs, lambda ti=ti, cbfv=cbfv, dcol=dcol: nc.scalar.activation(out=cbfv, in_=ctmp[ti][:], func=AF.Copy, scale=dcol))
                        if h % 2 == 1:
                            yield
                    for hp in range(2):
                        pso, pbo = pair[hp]
                        mi = rot("sm")
                        S_ = sm[mi]; smb = sm_b[mi]
                        for hh in range(2):
                            k.op("dve", [pbo], [smb], lambda S_=S_, pso=pso, hh=hh: V.bn_stats(out=S_[:, 6 * hh:6 * hh + 6], in_=pso[:, 256 * hh:256 * hh + 192]))
                        for hh in range(2):
                            k.op("dve", [smb], [smb], lambda S_=S_, hh=hh: V.bn_aggr(out=S_[:, 12 + 2 * hh:14 + 2 * hh], in_=S_[:, 6 * hh:6 * hh + 6]))
                        irp = ir_sb[:, 4 * c + 2 * hp:4 * c + 2 * hp + 2]
                        k.op("dve", [pbo, r_b, smb], [smb], lambda S_=S_, pso=pso, irp=irp: V.tensor_tensor(out=S_[:, 16:18], in0=pso[:, 192:449:256], in1=irp, op=ALU.max))
                        k.op("dve", [pbo, smb], [smb], lambda S_=S_, pso=pso: V.scalar_tensor_tensor(out=S_[:, 18:20], in0=pso[:, 192:449:256], scalar=-1.0, in1=S_[:, 16:18], op0=ALU.mult, op1=ALU.max))
                        k.op("dve", [smb], [smb], lambda S_=S_: V.scalar_tensor_tensor(out=S_[:, 20:22], in0=S_[:, 18:20], scalar=EPS, in1=S_[:, 18:20], op0=ALU.mult, op1=ALU.mult))
                        k.op("dve", [smb], [smb], lambda S_=S_: V.tensor_tensor(out=S_[:, 22:24], in0=S_[:, 20:22], in1=S_[:, 13:16:2], op=ALU.add))
                        k.op("act", [smb], [smb], lambda S_=S_: nc.scalar.activation(out=S_[:, 24:26], in_=S_[:, 22:24], func=AF.Sqrt))
                        k.op("dve", [smb], [smb], lambda S_=S_: V.reciprocal(out=S_[:, 26:28], in_=S_[:, 24:26]))
                        for hh in range(2):
                            h = 2 * hp + hh
                            yt = rot("ytmp")
                            k.op("dve", [pbo, smb], [ytmp_b[yt]], lambda S_=S_, pso=pso, yt=yt, hh=hh: V.tensor_scalar(
                                out=ytmp[yt][:], in0=pso[:, 256 * hh:256 * hh + 192], scalar1=S_[:, 12 + 2 * hh:13 + 2 * hh], scalar2=S_[:, 26 + hh:27 + hh], op0=ALU.subtract, op1=ALU.mult))
                            k.op("pool", [ytmp_b[yt], p["nwsh_b"]], [ytmp_b[yt]], lambda yt=yt, h=h: nc.gpsimd.tensor_tensor(
                                out=ytmp[yt][:], in0=ytmp[yt][:], in1=p["nwa"][:, 192 * h:192 * (h + 1)], op=ALU.mult))
                            k.op("pool", [ytmp_b[yt], gsa_b[c]], [ya_b[yi]], lambda yt=yt, h=h, yi=yi: nc.gpsimd.tensor_tensor(
                                out=ya[yi][:, 192 * h:192 * (h + 1)], in0=ytmp[yt][:], in1=gsa[:, c, 192 * h:192 * (h + 1)], op=ALU.mult))
                        yield
                    dbg("ya", ya[yi][:], [ya_b[yi]], view=lambda d_, c=c: d_[c * 128:(c + 1) * 128, :])
                    ps, pb = psum("misc")
                    psv = ps[:].bitcast(BF16)
                    k.grp("pe", [ya_b[yi], cstb_b], [pb], [
                        (lambda j=j, yi=yi, psv=psv: nc.tensor.transpose(psv[:, 128 * j:128 * (j + 1)], ya[yi][:, 128 * j:128 * (j + 1)], ident_b))
                        for j in range(6)])
                    k.op("act", [pb], [yaT_b], lambda c=c, psv=psv: nc.scalar.copy(out=yaT[:, :, c * 128:(c + 1) * 128], in_=psv[:, 0:768].rearrange("p (j t) -> p j t", j=6)))
                    yield
            pmode["wide"] = False
            if stage >= 3:
                interleave(_mloop(), s5_pre(l, t), 2)
            else:
                for _ in _mloop():
                    pass
            pmode["wide"] = True
            if stage >= 2:
                for _ in _gla_pre():
                    pass
            dbg("yaT", yaT[:], [yaT_b])
            if stage < 2:
                return
            k.mark("L%d_t%d_gloop" % (l, t))
            def _gloop():
                V = nc.vector
                tri4 = tri_f.unsqueeze(1).to_broadcast([128, 4, 128])
                for c in range(NCH):
                    tsl = slice(c * 128, (c + 1) * 128)
                    yi = rot("ya")
                    pss, pbs = psum("att")
                    k.grp("pe", kg_b + qg_b, [pbs], [
                        (lambda h=h, pss=pss: nc.tensor.matmul(pss[:, 128 * h:128 * (h + 1)], lhsT=kg[:, h, tsl], rhs=qg[:, h, tsl], start=True, stop=True))
                        for h in range(4)])
                    yield
                    si = rot("stm")
                    k.op("dve", [pbs, cst_b], [stm_b[si]], lambda si=si, pss=pss: V.tensor_tensor(
                        out=stm[si][:], in0=pss[:, 0:512].rearrange("p (h t) -> p h t", h=4), in1=tri4, op=ALU.mult))
                    yield
                    pair = []
                    for hp in range(2):
                        pso, pbo = psum("pair")
                        fns = []
                        for hh in range(2):
                            h = 2 * hp + hh
                            oc = 256 * hh
                            fns.append(lambda si=si, pso=pso, h=h, oc=oc: nc.tensor.matmul(pso[:, oc:oc + 192], lhsT=stm[si][:, h, :], rhs=vg[:, c, 192 * h:192 * (h + 1)], start=True, stop=False))
                            fns.append(lambda pso=pso, h=h, oc=oc: nc.tensor.matmul(pso[:, oc:oc + 192], lhsT=qg[:, h, tsl], rhs=p["Sbf"][:, h, :], start=False, stop=True))
                        k.grp("pe", [stm_b[si], vg_b[c]] + qg_b[2 * hp:2 * hp + 2] + p["S_b"][2 * hp:2 * hp + 2], [pbo], fns)
                        pair.append((pso, pbo))
                        yield
                    for hp in range(2):
                        psd, pbd = psum("att")
                        k.grp("pe", [ktokg_b[c], vg_b[c]], [pbd], [
                            (lambda hh=hh, psd=psd, hp=hp: nc.tensor.matmul(psd[0:96, 192 * hh:192 * (hh + 1)], lhsT=ktokg[:, c, 96 * (2 * hp + hh):96 * (2 * hp + hh + 1)],
                                                                           rhs=vg[:, c, 192 * (2 * hp + hh):192 * (2 * hp + hh + 1)], start=True, stop=True))
                            for hh in range(2)])
                        yield
                        ti = rot("ctmp")
                        sbs = p["S_b"][2 * hp:2 * hp + 2]
                        s32v = p["S32"][:, 2 * hp:2 * hp + 2, :].rearrange("p a e -> p (a e)")
                        k.op("dve", [pbd] + sbs, [ctmp_b[ti]], lambda psd=psd, ti=ti, s32v=s32v: V.tensor_tensor(out=ctmp[ti][:, 0:384], in0=psd[0:96, 0:384], in1=s32v, op=ALU.add))
                        yield
                        for hh in range(2):
                            h = 2 * hp + hh
                            k.op("dve", [ctmp_b[ti], dgl_b[h]], [p["S_b"][h]], lambda ti=ti, h=h, hh=hh, c=c: V.tensor_scalar(
                                out=p["S32"][:, h, :], in0=ctmp[ti][:, 192 * hh:192 * (hh + 1)], scalar1=dgl[:, h, c:c + 1], scalar2=None, op0=ALU.mult))
                            yield
                            k.op("act", [ctmp_b[ti], dgl_b[h]], [p["S_b"][h]], lambda ti=ti, h=h, hh=hh, c=c: nc.scalar.activation(
                                out=p["Sbf"][:, h, :], in_=ctmp[ti][:, 192 * hh:192 * (hh + 1)], func=AF.Copy, scale=dgl[:, h, c:c + 1]))
                            yield
                    mi = rot("sm")
                    S_ = sm[mi]; smb = sm_b[mi]
                    for h in range(4):
                        pso, pbo = pair[h // 2]
                        k.op("dve", [pbo], [smb], lambda S_=S_, pso=pso, h=h: V.bn_stats(out=S_[:, 6 * h:6 * h + 6], in_=pso[:, 256 * (h % 2):256 * (h % 2) + 192]))
                        yield
                    for h in range(4):
                        k.op("dve", [smb], [smb], lambda S_=S_, h=h: V.bn_aggr(out=S_[:, 24 + 2 * h:26 + 2 * h], in_=S_[:, 6 * h:6 * h + 6]))
                        yield
                    k.op("dve", [smb], [smb], lambda S_=S_: V.tensor_tensor(out=S_[:, 32:36], in0=S_[:, 24:32:2], in1=S_[:, 24:32:2], op=ALU.mult))
                    yield
                    k.op("dve", [smb], [smb], lambda S_=S_: V.tensor_tensor(out=S_[:, 36:40], in0=S_[:, 32:36], in1=S_[:, 25:32:2], op=ALU.add))
                    yield
                    k.op("act", [smb], [smb], lambda S_=S_: nc.scalar.activation(out=S_[:, 40:44], in_=S_[:, 36:40], func=AF.Sqrt, bias=EPS, scale=1.0))
                    yield
                    k.op("dve", [smb], [smb], lambda S_=S_: V.reciprocal(out=S_[:, 44:48], in_=S_[:, 40:44]))
                    yield
                    for h in range(4):
                        pso, pbo = pair[h // 2]
                        yt = rot("ytmp")
                        k.op("dve", [pbo, smb, p["nwsh_b"]], [ytmp_b[yt]], lambda S_=S_, pso=pso, yt=yt, h=h: V.scalar_tensor_tensor(
                            out=ytmp[yt][:], in0=pso[:, 256 * (h % 2):256 * (h % 2) + 192], scalar=S_[:, 44 + h:45 + h], in1=p["nwb"][:, 192 * h:192 * (h + 1)], op0=ALU.mult, op1=ALU.mult))
                        yield
                        k.op("pool", [ytmp_b[yt], gsb_b[c]], [ya_b[yi]], lambda yt=yt, h=h, yi=yi, c=c: nc.gpsimd.tensor_tensor(
                            out=ya[yi][:, 192 * h:192 * (h + 1)], in0=ytmp[yt][:], in1=gsb[:, c, 192 * h:192 * (h + 1)], op=ALU.mult))
                        yield
                    dbg("yb", ya[yi][:], [ya_b[yi]], view=lambda d_, c=c: d_[c * 128:(c + 1) * 128, :])
                    ps, pb = psum("misc")
                    psv = ps[:].bitcast(BF16)
                    k.grp("pe", [ya_b[yi], cstb_b], [pb], [
                        (lambda j=j, yi=yi, psv=psv: nc.tensor.transpose(psv[:, 128 * j:128 * (j + 1)], ya[yi][:, 128 * j:128 * (j + 1)], ident_b))
                        for j in range(6)])
                    k.op("act", [pb], [ybT_b], lambda c=c, psv=psv: nc.scalar.copy(out=ybT[:, :, c * 128:(c + 1) * 128], in_=psv[:, 0:768].rearrange("p (j t) -> p j t", j=6)))
                    yield
            pmode["wide"] = False
            if stage >= 3:
                interleave(_gloop(), chain2(s5_scan(l, t), s5_stage(l, t)), 3)
            else:
                for _ in _gloop():
                    pass
            pmode["wide"] = True
            if stage < 3:
                return
            k.mark("L%d_t%d_s5" % (l, t))
            if stage < 4:
                return
            k.mark("L%d_t%d_merge" % (l, t))
            merge_stage(l, t)
            k.mark("L%d_t%d_end" % (l, t))

        def s5_pre(l, t):
            p = P[l]
            tb = s5t["b"]
            V = nc.vector
            for fc in range(4):
                if fc % 2 == 0:
                    wt, wb = wload(l, "cz%d" % (fc // 2))
                ps, pb = proj_fm(wt, wb, 256, 128 * (fc % 2), 128)
                k.op("act", [pb], [czs_b[fc]], lambda ps=ps, fc=fc: nc.scalar.activation(out=czs[:, fc, :], in_=ps[:, 0:TT], func=AF.Silu))
                yield
            k.mark("L%d_t%d_s5a_czdone" % (l, t))
            for half in range(2):
                wt, wb = wload(l, "cu%d" % half)
                wv = wt[:, 0:8 * 256].rearrange("p (k n) -> p k n", k=8)
                for sp_ in range(4):
                    ps, pb = psum("proj")
                    fns = []
                    for s2 in range(2):
                        s_ = 2 * sp_ + s2
                        for kk in range(8):
                            fns.append(lambda ps=ps, s_=s_, s2=s2, kk=kk, wv=wv: nc.tensor.matmul(
                                ps[0:JJ, 256 * s2:256 * (s2 + 1)], lhsT=hn[:, kk, s_::8], rhs=wv[:, kk, :], start=(kk == 0), stop=(kk == 7)))
                    k.grp("pe", [wb, hn_b], [pb], fns)
                    k.op("act", [pb, tb], [Ut_b], lambda ps=ps, sp_=sp_, half=half: nc.scalar.copy(
                        out=Ut[:, 16 * half:16 * half + 16, 2 * sp_:2 * sp_ + 2, :], in_=ps[0:JJ, 0:512].rearrange("p (s g h) -> p g s h", s=2, g=16)))
                    yield
            k.mark("L%d_t%d_s5b_utdone" % (l, t))
            ps, pb = psum("misc")
            psv = ps[:].bitcast(BF16)
            k.grp("pe", [Ut_b, cstb_b], [pb], [
                (lambda g=g, psv=psv: nc.tensor.transpose(psv[:, JJ * g:JJ * (g + 1)], Ut[:, g, :, :], ident_b[0:JJ, 0:JJ]))
                for g in range(32)])
            k.op("act", [pb], [U_b], lambda psv=psv: nc.scalar.copy(out=U_all[:, 0:16, :].rearrange("p g j -> p (g j)"), in_=psv[:, 0:16 * JJ]))
            k.op("dve", [pb], [U_b], lambda psv=psv: V.tensor_copy(out=U_all[:, 16:32, :].rearrange("p g j -> p (g j)"), in_=psv[:, 16 * JJ:32 * JJ]))
            yield
            k.mark("L%d_t%d_s5c_ualldone" % (l, t))
            k.op("dve", [p["Xc_b"], tb], [Xall_b], lambda: V.tensor_copy(out=Xall[:, 0, :, :], in_=p["Xc"][:]))
            for ri in range(2):
                for gh in range(2):
                    ps, pb = psum("proj")
                    k.grp("pe", [shb, U_b], [pb], [
                        (lambda ps=ps, ri=ri, g=g, gi=gi: nc.tensor.matmul(ps[0:64, JJ * gi:JJ * (gi + 1)], lhsT=G_sh[:, g, ri, :], rhs=U_all[:, g, :], start=True, stop=True))
                        for gi, g in enumerate(range(16 * gh, 16 * gh + 16))])
                    k.op("act", [pb, tb], [Xall_b], lambda ps=ps, ri=ri, gh=gh: nc.scalar.copy(
                        out=Xall[:, 1:JJ + 1, ri, 16 * gh:16 * gh + 16], in_=ps[0:64, 0:16 * JJ].rearrange("p (g j) -> p j g", g=16)))
                    yield
            yield

        def s5_scan(l, t):
            p = P[l]
            tb = s5t["b"]
            V = nc.vector
            k.mark("L%d_t%d_s5d_wdone" % (l, t))
            rd = [Xall_b, p["b"]]
            for j in range(JJ):
                k.op("dve", rd, [st1_b], lambda j=j: V.tensor_tensor(out=st1[:], in0=Xall[:, j, :, :], in1=p["AR8"][:], op=ALU.mult))
                yield
                k.op("dve", rd, [st2a_b], lambda j=j: V.tensor_tensor(out=st2[:, 0, :], in0=Xall[:, j, 1, :], in1=p["AI8"][:, 0, :], op=ALU.mult))
                yield
                k.op("dve", rd, [st2b_b], lambda j=j: V.tensor_tensor(out=st2[:, 1, :], in0=Xall[:, j, 0, :], in1=p["AI8"][:, 1, :], op=ALU.mult))
                yield
                k.op("dve", [Xall_b, st1_b], [Xall_b], lambda j=j: V.tensor_tensor(out=Xall[:, j + 1, :, :], in0=Xall[:, j + 1, :, :], in1=st1[:], op=ALU.add))
                yield
                k.op("dve", [Xall_b, st2a_b, st2b_b], [Xall_b], lambda j=j: V.tensor_tensor(out=Xall[:, j + 1, :, :], in0=Xall[:, j + 1, :, :], in1=st2[:], op=ALU.add))
                yield
            k.op("dve", [Xall_b], [p["Xc_b"]], lambda: V.tensor_copy(out=p["Xc"][:], in_=Xall[:, JJ, :, :]))
            for ri in range(2):
                k.op("act", [Xall_b], [xbf_b], lambda ri=ri: nc.scalar.copy(out=xbf[:, ri, :, :], in_=Xall[:, 0:JJ, ri, :].rearrange("p j g -> p g j")))
            k.mark("L%d_t%d_s5e_scandone" % (l, t))
            yield

        def s5_stage(l, t):
            p = P[l]
            tb = s5t["b"]
            V = nc.vector
            for blk in range(8):
                ps, pb = psum("att")
                fns = []
                for gi in range(4):
                    g = 4 * blk + gi
                    fns.append(lambda ps=ps, g=g, gi=gi: nc.tensor.matmul(ps[0:JJ, 128 * gi:128 * (gi + 1)], lhsT=U_all[:, g, :], rhs=Tz_sh[:, g, :], start=True, stop=False))
                    fns.append(lambda ps=ps, g=g, gi=gi: nc.tensor.matmul(ps[0:JJ, 128 * gi:128 * (gi + 1)], lhsT=xbf[:, 0, g, :], rhs=H_sh[:, g, 0, :], start=False, stop=False))
                    fns.append(lambda ps=ps, g=g, gi=gi: nc.tensor.matmul(ps[0:JJ, 128 * gi:128 * (gi + 1)], lhsT=xbf[:, 1, g, :], rhs=H_sh[:, g, 1, :], start=False, stop=True))
                k.grp("pe", [U_b, shb, hsh_b, xbf_b], [pb], fns)
                eng = "act" if blk % 2 == 0 else "dve"
                if eng == "act":
                    k.op("act", [pb, tb], [Yt_b], lambda ps=ps, blk=blk: nc.scalar.copy(
                        out=Yt[:, :, 4 * blk:4 * blk + 4, :], in_=ps[0:JJ, 0:512].rearrange("p (g t h) -> p t g h", g=4, t=8)))
                else:
                    k.op("dve", [pb, tb], [Yt_b], lambda ps=ps, blk=blk: V.tensor_copy(
                        out=Yt[:, :, 4 * blk:4 * blk + 4, :], in_=ps[0:JJ, 0:512].rearrange("p (g t h) -> p t g h", g=4, t=8)))
                yield
            k.mark("L%d_t%d_s5f_ydone" % (l, t))
            for fc in range(4):
                ps, pb = psum("misc")
                psv = ps[:].bitcast(BF16)
                k.grp("pe", [Yt_b, cstb_b], [pb], [
                    (lambda t_=t_, fc=fc, psv=psv: nc.tensor.transpose(psv[:, JJ * t_:JJ * (t_ + 1)], Yt[:, t_, 8 * fc:8 * (fc + 1), :], ident_b[0:JJ, 0:JJ]))
                    for t_ in range(8)])
                k.op("act", [pb], [yc0_b[fc]], lambda fc=fc, psv=psv: nc.scalar.copy(
                    out=yc0[:, fc, :].rearrange("p (j t) -> p t j", t=8), in_=psv[:, 0:8 * JJ].rearrange("p (t j) -> p t j", t=8)))
                yield
            k.mark("L%d_t%d_s5g_trdone" % (l, t))
            dbg("s5", yc0[:], yc0_b)
            for fc in range(4):
                k.op("act", [yc0_b[fc]], [yc0_b[fc]], lambda fc=fc: nc.scalar.activation(out=yc0[:, fc, :], in_=yc0[:, fc, :], func=AF.Gelu))
                k.op("dve", [yc0_b[fc]], [yc0_b[fc]], lambda fc=fc: nc.vector.tensor_copy(out=yc0b[:, fc, :], in_=yc0[:, fc, :]))
                yield
            wt, wb = wload(l, "glu")
            wv = wt[:, 0:4 * 512].rearrange("p (k n) -> p k n", k=4)
            for fc in range(4):
                ps, pb = psum("proj")
                k.grp("pe", [wb] + yc0_b, [pb], [
                    (lambda ps=ps, kc=kc, fc=fc: nc.tensor.matmul(ps[:, 0:TT], lhsT=wv[:, kc, 128 * fc:128 * (fc + 1)], rhs=yc0b[:, kc, :], start=(kc == 0), stop=(kc == 3)))
                    for kc in range(4)])
                mi = rot("mg")
                k.op("act", [pb], [mg_b[mi]], lambda ps=ps, mi=mi: nc.scalar.activation(out=mg[mi][:], in_=ps[:, 0:TT], func=AF.Sigmoid))
                k.op("dve", [mg_b[mi], yc0_b[fc]], [mg_b[mi]], lambda mi=mi, fc=fc: nc.vector.tensor_tensor(out=mg[mi][:], in0=mg[mi][:], in1=yc0[:, fc, :], op=ALU.mult))
                k.op("dve", [mg_b[mi], czs_b[fc]], [ycT_b[fc]], lambda mi=mi, fc=fc: nc.vector.tensor_tensor(out=ycT[:, fc, :], in0=mg[mi][:], in1=czs[:, fc, :], op=ALU.mult))
                yield
            dbg("ycT", ycT[:], ycT_b)
            yield

        def merge_stage(l, t):
            p = P[l]
            for j in range(8):
                wtg, wbg = wload(l, "g%d" % j)
                wtb, wbb = wload(l, "br%d" % j)
                si = rot("sg")
                for i in range(3):
                    ps, pb = proj_fm(wtg[:, 1024 * i:1024 * (i + 1)], wbg, 128, 0, 128)
                    k.op("act", [pb], [sg_b[si]], lambda ps=ps, si=si, i=i: nc.scalar.activation(out=sg[si][:, i, :], in_=ps[:, 0:TT], func=AF.Sigmoid))
                mi = rot("mg")
                off = 0
                for i, (src, src_bufs, kch) in enumerate([(yaT, [yaT_b], 6), (ybT, [ybT_b], 6), (ycT, ycT_b, 4)]):
                    wv = wtb[:, off:off + kch * 128].rearrange("p (k n) -> p k n", k=kch)
                    off += kch * 128
                    ps, pb = psum("proj")
                    k.grp("pe", [wbb] + src_bufs, [pb], [
                        (lambda ps=ps, kc=kc, wv=wv, src=src, kch=kch: nc.tensor.matmul(ps[:, 0:TT], lhsT=wv[:, kc, :], rhs=src[:, kc, :], start=(kc == 0), stop=(kc == kch - 1)))
                        for kc in range(kch)])
                    if i == 0:
                        k.op("dve", [pb, sg_b[si]], [mg_b[mi]], lambda ps=ps, si=si, mi=mi, i=i: nc.vector.tensor_tensor(out=mg[mi][:], in0=ps[:, 0:TT], in1=sg[si][:, i, :], op=ALU.mult))
                    else:
                        k.op("dve", [pb, sg_b[si]], [sg_b[si]], lambda ps=ps, si=si, i=i: nc.vector.tensor_tensor(out=sg[si][:, i, :], in0=ps[:, 0:TT], in1=sg[si][:, i, :], op=ALU.mult))
                        if i == 1:
                            k.op("pool", [sg_b[si], mg_b[mi]], [mg_b[mi]], lambda si=si, mi=mi, i=i: nc.gpsimd.tensor_tensor(out=mg[mi][:], in0=mg[mi][:], in1=sg[si][:, i, :], op=ALU.add))
                        else:
                            k.op("pool", [sg_b[si], mg_b[mi]], [mT_b[j]], lambda si=si, mi=mi, i=i, j=j: nc.gpsimd.tensor_tensor(out=mT[:, j, :], in0=mg[mi][:], in1=sg[si][:, i, :], op=ALU.add))
            dbg("mT", mT[:], mT_b)
            for h2 in range(4):
                wt, wb = wload(l, "wo%d" % h2)
                wv = wt[:, 0:8 * 256].rearrange("p (k n) -> p k n", k=8)
                for jj in range(2):
                    j = 2 * h2 + jj
                    ps, pb = psum("proj")
                    k.grp("pe", [wb] + mT_b, [pb], [
                        (lambda ps=ps, kc=kc, jj=jj, wv=wv: nc.tensor.matmul(ps[:, 0:TT], lhsT=wv[:, kc, 128 * jj:128 * (jj + 1)], rhs=mT[:, kc, :], start=(kc == 0), stop=(kc == 7)))
                        for kc in range(8)])
                    k.op("dve", [pb, x_b], [x_b], lambda ps=ps, j=j: nc.vector.tensor_tensor(out=x_sb[:, j, :], in0=x_sb[:, j, :], in1=ps[:, 0:TT], op=ALU.add))

        xT_v = xT_d.rearrange("(k p) t -> p k t", p=128)
        oT_v = outT_d.rearrange("(k p) t -> p k t", p=128)
        out_b = Buf("out")
        for t in range(NT):
            k.dma("sp", [s5t["b"]] if "b" in s5t else [], [x_b], lambda t=t: nc.sync.dma_start(out=x_sb[:], in_=xT_v[:, :, t * TT:(t + 1) * TT]))
            for l in range(NL):
                layer(l, t)
            dbg("xo", x_sb[:], [x_b])
            norm_stage(0, None, None)
            for kk in range(8):
                oi = 0
                k.op("dve", [x_b, rstd_b, fnw_b], [o_b[oi]], lambda kk=kk, oi=oi: nc.vector.scalar_tensor_tensor(
                    out=o_sb[oi][:], in0=x_sb[:, kk, :], scalar=fnw[:, kk:kk + 1], in1=rstd[:], op0=ALU.mult, op1=ALU.mult))
                k.dma("sp", [o_b[oi]], [out_b], lambda t=t, kk=kk, oi=oi: nc.sync.dma_start(out=oT_v[:, kk, t * TT:(t + 1) * TT], in_=o_sb[oi][:]))
        k.final_wait("pool", [out_b] + dbg_bufs)
        print("instructions:", k.n_inst, "counts", k.cnt)
        build.marks = k.marks
    return nc


_NC_CACHE = {}


def kernel(**inputs):
    x = np.asarray(inputs["x"], dtype=np.float32)
    B, T, _ = x.shape
    if T not in _NC_CACHE:
        _NC_CACHE[T] = build(T, NL=2, stage=4)
    nc = _NC_CACHE[T]
    bm = host_masks()
    cst = host_consts()
    in_maps = []
    for b in range(B):
        im = {n: np.ascontiguousarray(np.asarray(inputs[n], dtype=np.float32)) for n in WSHAPES}
        im["xT"] = np.ascontiguousarray(x[b].T)
        im["consts"] = cst
        im["bmask"] = bm
        in_maps.append(im)
    res = run_bass_kernel_spmd(nc, in_maps, core_ids=list(range(B)))
    out = np.stack([np.ascontiguousarray(res.results[b]["outT"].T) for b in range(B)], axis=0)
    return out.astype(np.float32)
```

```python
import numpy as np
from contextlib import ExitStack
import concourse.bass as bass
import concourse.mybir as mybir
from concourse.bass_utils import run_bass_kernel_spmd

F32 = mybir.dt.float32
BF16 = mybir.dt.bfloat16
ALU = mybir.AluOpType
AF = mybir.ActivationFunctionType

D = 1024
INW = 10264
NDS = 24
NDS_SW = 8


class Buf:
    __slots__ = ("w", "r", "name")

    def __init__(self, name=""):
        self.w = None
        self.r = {}
        self.name = name


class K:
    def __init__(self, nc, es, strict=True):
        self.nc = nc
        self.es = es
        self.strict = strict
        self.eng = {"pe": nc.tensor, "act": nc.scalar, "dve": nc.vector, "pool": nc.gpsimd, "sp": nc.sync}
        self.semobjs = []
        self.esem = {}
        for e in ["pe", "act", "dve", "pool"]:
            self.esem[e] = len(self.semobjs)
            self.semobjs.append(es.enter_context(nc.semaphore("s_" + e)))
        self.cnt = {e: 0 for e in self.esem}
        self.seen = {e: {} for e in self.eng}
        self.dsem = []
        for i in range(NDS):
            self.dsem.append(len(self.semobjs))
            self.semobjs.append(es.enter_context(nc.semaphore("d%d" % i)))
        self.dcnt = [0] * NDS
        self.dnext = 0
        self.dnext_sw = 0
        self.n_inst = 0
        self.n_eng = {e: 0 for e in self.eng}
        self.marks = []

    def _wait(self, e, toks, relax=False):
        need = {}
        for t in toks:
            if t is None:
                continue
            key, val, te = t
            if te == e and e == "pe":
                continue
            if self.seen[e].get(key, 0) >= val:
                continue
            if need.get(key, 0) < val:
                need[key] = val
        for key, val in need.items():
            self.eng[e].wait_ge(self.semobjs[key], val)
            self.seen[e][key] = val

    def _deps(self, reads, writes):
        deps = []
        for b in reads:
            if b.w is not None:
                deps.append(b.w)
        for b in writes:
            if b.w is not None:
                deps.append(b.w)
            for k_, (v_, te_) in b.r.items():
                deps.append((k_, v_, te_))
        return deps

    def _commit(self, tok, reads, writes):
        for b in reads:
            if b.r.get(tok[0], (0, None))[0] < tok[1]:
                b.r[tok[0]] = (tok[1], tok[2])
        for b in writes:
            b.w = tok
            b.r = {}

    def op(self, e, reads, writes, fn, relax=False):
        self._wait(e, self._deps(reads, writes), relax)
        ins = fn()
        self.n_eng[e] += 1
        self.cnt[e] += 1
        ins.then_inc(self.semobjs[self.esem[e]], 1)
        tok = (self.esem[e], self.cnt[e], e)
        self._commit(tok, reads, writes)
        self.n_inst += 1
        return tok

    def grp(self, e, reads, writes, fns):
        self._wait(e, self._deps(reads, writes))
        ins = None
        for fn in fns:
            ins = fn()
            self.n_inst += 1
            self.n_eng[e] += 1
        self.cnt[e] += 1
        ins.then_inc(self.semobjs[self.esem[e]], 1)
        tok = (self.esem[e], self.cnt[e], e)
        self._commit(tok, reads, writes)
        return tok

    def dma(self, q, reads, writes, fn):
        if q == "pool":
            i = self.dnext_sw
            self.dnext_sw = (self.dnext_sw + 1) % NDS_SW
        else:
            i = NDS_SW + self.dnext
            self.dnext = (self.dnext + 1) % (NDS - NDS_SW)
        deps = self._deps(reads, writes)
        if self.dcnt[i] > 0:
            deps.append((self.dsem[i], self.dcnt[i], None))
        self._wait(q, deps)
        ins = fn()
        self.dcnt[i] += 16
        ins.then_inc(self.semobjs[self.dsem[i]], 16)
        tok = (self.dsem[i], self.dcnt[i], None)
        self._commit(tok, reads, writes)
        self.n_inst += 1
        return tok

    def final_wait(self, q, bufs):
        toks = []
        for b in bufs:
            if b.w is not None:
                toks.append(b.w)
        self._wait(q, toks)

    def mark(self, label):
        self.marks.append((label, dict(self.n_eng)))


TT = 256
NCH = TT // 128
SEG = dict(aq=0, ak=768, av=1536, ao=2304, ai=3072, af=3076, az=3080, bq=3848, bk=4232, bv=4616,
           ba=5384, bz=5400, cu=6168, cz=6680, g=7192)
WSHAPES = dict(
    norm_w=[2, 1024], w_in=[2, 1024, INW], mlstm_conv=[2, 4, 1536], mlstm_gate_b=[2, 2, 4],
    mlstm_norm=[2, 768], gla_w_alpha=[2, 16, 384], gla_b_alpha=[2, 384], gla_norm=[2, 768],
    s5_lam_re=[2, 32, 64], s5_lam_im=[2, 32, 64], s5_log_dt=[2, 32], s5_B_re=[2, 32, 64, 16],
    s5_B_im=[2, 32, 64, 16], s5_C_re=[2, 32, 16, 64], s5_C_im=[2, 32, 16, 64], s5_D=[2, 512],
    s5_w_glu=[2, 512, 512], w_branch_mlstm=[2, 768, 1024], w_branch_gla=[2, 768, 1024],
    w_branch_s5=[2, 512, 1024], w_out=[2, 1024, 1024], final_norm=[1024])
EPS = 1e-6
SA = 192 ** -0.5
SB_ = 96 ** -0.5


def host_masks():
    bm = np.zeros((128, 128), np.float32)
    for s_ in range(8):
        for t_ in range(s_, 8):
            bm[16 * s_:16 * (s_ + 1), 16 * t_:16 * (t_ + 1)] = 1.0
    return bm


def host_consts():
    c = np.zeros((128, 3, 128), np.float32)
    c[:, 0, :] = np.eye(128, dtype=np.float32)
    c[:, 1, :] = np.triu(np.ones((128, 128), np.float32))
    c[:, 2, :] = 1.0
    return c


def build(T, NL=2, stage=99, dbgspec=None):
    nc = bass.Bass("TRN2", target_bir_lowering=False)
    NT = T // TT
    dbgspec = dbgspec or {}
    with ExitStack() as es:
        es.enter_context(nc.allow_non_contiguous_dma(reason="small param loads"))
        k = K(nc, es)
        xT_d = nc.dram_tensor("xT", [D, T], F32, kind="ExternalInput").ap()
        cst_d = nc.dram_tensor("consts", [128, 3, 128], F32, kind="ExternalInput").ap()
        bm_d = nc.dram_tensor("bmask", [128, 128], F32, kind="ExternalInput").ap()
        W = {n: nc.dram_tensor(n, s, F32, kind="ExternalInput").ap() for n, s in WSHAPES.items()}
        outT_d = nc.dram_tensor("outT", [D, T], F32, kind="ExternalOutput").ap()
        dbg_d = {n: nc.dram_tensor("dbg_" + n, list(s), F32, kind="ExternalOutput").ap() for n, s in dbgspec.items()}
        dbg_bufs = []

        def sb(name, shape, dt=F32):
            return es.enter_context(nc.sbuf_tensor(name, shape, dt))

        blocks = {}

        def def_block(l, key, parts):
            tot = sum((p.shape[0] // 128) * p.shape[1] for p in parts)
            dt_ = nc.dram_tensor("wb_%d_%s" % (l, key), [128, tot], BF16, kind="Internal").ap()
            b = Buf("wb")
            off = 0
            for p in parts:
                kch, n = p.shape[0] // 128, p.shape[1]
                src = p.rearrange("(k p) n -> p k n", p=128)
                dst = dt_[:, off:off + kch * n].rearrange("p (k n) -> p k n", k=kch)
                k.dma("pool", [], [b], lambda src=src, dst=dst: nc.gpsimd.dma_start(out=dst, in_=src))
                off += kch * n
            blocks[(l, key)] = (dt_, b, tot)

        for l in range(NL):
            wi = W["w_in"][l]
            for nm in ["aq", "ak"]:
                for h in range(2):
                    def_block(l, "%s%d" % (nm, h), [wi[:, SEG[nm] + 384 * h: SEG[nm] + 384 * (h + 1)]])
            def_block(l, "aif", [wi[:, SEG["ai"]:SEG["ai"] + 8]])
            for nm in ["av", "ao", "az"]:
                for h in range(2):
                    def_block(l, "%s%d" % (nm, h), [wi[:, SEG[nm] + 384 * h: SEG[nm] + 384 * (h + 1)]])
            def_block(l, "bq", [wi[:, SEG["bq"]:SEG["bq"] + 384]])
            def_block(l, "bk", [wi[:, SEG["bk"]:SEG["bk"] + 384]])
            def_block(l, "ba", [wi[:, SEG["ba"]:SEG["ba"] + 16]])
            for nm in ["bv", "bz"]:
                for h in range(2):
                    def_block(l, "%s%d" % (nm, h), [wi[:, SEG[nm] + 384 * h: SEG[nm] + 384 * (h + 1)]])
            for h in range(2):
                def_block(l, "cu%d" % h, [wi[:, SEG["cu"] + 256 * h:SEG["cu"] + 256 * (h + 1)]])
                def_block(l, "cz%d" % h, [wi[:, SEG["cz"] + 256 * h:SEG["cz"] + 256 * (h + 1)]])
            def_block(l, "glu", [W["s5_w_glu"][l]])
            for j in range(8):
                def_block(l, "g%d" % j, [wi[:, SEG["g"] + 1024 * i + 128 * j: SEG["g"] + 1024 * i + 128 * (j + 1)] for i in range(3)])
                def_block(l, "br%d" % j, [W["w_branch_mlstm"][l][:, 128 * j:128 * (j + 1)],
                                          W["w_branch_gla"][l][:, 128 * j:128 * (j + 1)],
                                          W["w_branch_s5"][l][:, 128 * j:128 * (j + 1)]])
            for h in range(4):
                def_block(l, "wo%d" % h, [W["w_out"][l][:, 256 * h:256 * (h + 1)]])

        NRING = 3
        ring = [sb("wring%d" % i, [128, 3072], BF16) for i in range(NRING)]
        ring_b = [Buf("ring%d" % i) for i in range(NRING)]
        rstate = [0]

        def wload(l, key):
            dt_, b, tot = blocks[(l, key)]
            i = rstate[0]
            rstate[0] = (i + 1) % NRING
            k.dma("sp", [b] + ([s5t["b"]] if "b" in s5t else []), [ring_b[i]], lambda: nc.sync.dma_start(out=ring[i][:, 0:tot], in_=dt_))
            return ring[i], ring_b[i]

        cst = sb("cst", [128, 3, 128])
        cst_b = Buf("cst")
        k.dma("sp", [], [cst_b], lambda: nc.sync.dma_start(out=cst[:], in_=cst_d))
        cstb = sb("cstb", [128, 3, 128], BF16)
        cstb_b = Buf("cstb")
        k.op("dve", [cst_b], [cstb_b], lambda: nc.vector.tensor_copy(out=cstb[:], in_=cst[:]))
        ident_f, tri_f, ones_f = cst[:, 0, :], cst[:, 1, :], cst[:, 2, :]
        ident_b, tri_b = cstb[:, 0, :], cstb[:, 1, :]

        psf = [es.enter_context(nc.psum_tensor("ps%d" % i, [128, 512], F32)) for i in range(8)]
        ps_b = [Buf("ps%d" % i) for i in range(8)]
        pstate = {"proj": 0, "att": 0, "misc": 0, "pair": 0, "wide": 0}
        PPOOL = {"proj": [0, 1], "att": [3, 4], "misc": [6, 7], "pair": [2, 5], "wide": [0, 1, 3, 4, 2, 5]}
        pmode = {"wide": True}

        def psum(cls):
            if cls == "proj" and pmode["wide"]:
                cls = "wide"
            lst = PPOOL[cls]
            i = lst[pstate[cls] % len(lst)]
            pstate[cls] += 1
            return psf[i], ps_b[i]

        def dbg(name, ap_sb, bufs, view=None):
            if name not in dbg_d:
                return
            db = Buf("dbg")
            dst = dbg_d[name] if view is None else view(dbg_d[name])
            k.dma("pool", bufs, [db], lambda: nc.gpsimd.dma_start(out=dst, in_=ap_sb))
            dbg_bufs.append(db)

        x_sb = sb("x_sb", [128, 8, TT]); x_b = Buf("x")
        JJ = TT // 8
        S5W = 4160
        s5w = sb("s5w", [128, S5W])
        s5t = {}
        Tz_sh = sb("Tz_sh", [128, 32, 128], BF16)
        G_sh = sb("G_sh", [128, 32, 2, 64], BF16)
        H_sh = sb("H_sh", [64, 32, 2, 128], BF16)
        shb = Buf("s5sh"); hsh_b = Buf("s5hsh")
        mg = [sb("mg%d" % i, [128, TT]) for i in range(2)]; mg_b = [Buf("mg") for i in range(2)]
        tzt_b = mg_b
        bmask = sb("bmask_sb", [128, 128])
        k.dma("sp", [], [cst_b], lambda: nc.sync.dma_start(out=bmask[:], in_=bm_d))

        def s5_setup(l, p):
            pb = p["b"]
            V = nc.vector
            if not s5t:
                off = [0]

                def carve(n, parts=64):
                    ap_ = s5w[0:parts, off[0]:off[0] + n]
                    off[0] += n
                    return ap_
                for nm in ["lr", "li", "dtv", "mag", "ang", "t0", "t1", "t2", "cosv", "sinv", "ar", "ai", "nr", "den", "cre", "cim"]:
                    s5t[nm] = carve(32)
                for nm in ["bbre", "bbim", "Ccre", "Ccim"]:
                    s5t[nm] = carve(512).rearrange("p (g h) -> p g h", g=32)
                s5t["Apos"] = carve(576).rearrange("p (n r g) -> p n r g", n=9, r=2)
                s5t["Aneg"] = carve(512).rearrange("p (n r g) -> p n r g", n=8, r=2)
                s5t["Cld"] = s5w[:, off[0]:off[0] + 512].rearrange("p (r c q) -> p r c q", r=2, c=4)
                off[0] += 512
                assert off[0] <= S5W
                r0 = ring[0][0:64, :].bitcast(F32); r1 = ring[1][0:64, :].bitcast(F32); r2 = ring[2][0:64, :].bitcast(F32)
                xs = x_sb[0:64].rearrange("p k t -> p (k t)")
                s5t["Bre"] = r2[:, 0:512].rearrange("p (g h) -> p g h", g=32)
                s5t["Bim"] = r2[:, 512:1024].rearrange("p (g h) -> p g h", g=32)
                s5t["tA"] = xs[:, 0:512].rearrange("p (g h) -> p g h", g=32)
                s5t["tB"] = xs[:, 512:1024].rearrange("p (g h) -> p g h", g=32)
                o2 = [0]
                for nm in ["Lre", "Lim", "GTre", "GTim"]:
                    s5t[nm] = r0[:, o2[0]:o2[0] + 256].rearrange("p (g s h) -> p g s h", g=2, s=8)
                    o2[0] += 256
                s5t["Rre"] = r0[:, o2[0]:o2[0] + 288].rearrange("p (g s h) -> p g s h", g=2, s=9)
                o2 = [0]
                for nm in ["nRim", "u1", "u2"]:
                    s5t[nm] = r1[:, o2[0]:o2[0] + 288].rearrange("p (g s h) -> p g s h", g=2, s=9)
                    o2[0] += 288
                s5t["b"] = Buf("s5tmp")
            tb = s5t["b"]
            T_ = s5t

            def dv(fn):
                k.op("dve", [tb, cst_b], [tb], fn)

            def cmul(ore, oim, are, aim, bre, bim, t1, t2, neg_im=False):
                dv(lambda: V.tensor_tensor(out=t1, in0=are, in1=bre, op=ALU.mult))
                dv(lambda: V.tensor_tensor(out=t2, in0=aim, in1=bim, op=ALU.mult))
                dv(lambda: V.tensor_tensor(out=ore, in0=t1, in1=t2, op=ALU.subtract))
                dv(lambda: V.tensor_tensor(out=t1, in0=are, in1=bim, op=ALU.mult))
                dv(lambda: V.tensor_tensor(out=t2, in0=aim, in1=bre, op=ALU.mult))
                if neg_im:
                    dv(lambda: V.scalar_tensor_tensor(out=oim, in0=t1, scalar=-1.0, in1=t2, op0=ALU.mult, op1=ALU.subtract))
                else:
                    dv(lambda: V.tensor_tensor(out=oim, in0=t1, in1=t2, op=ALU.add))

            k.dma("sp", [], [tb], lambda: nc.sync.dma_start(out=T_["lr"], in_=W["s5_lam_re"][l].rearrange("g p -> p g")))
            k.dma("sp", [], [tb], lambda: nc.sync.dma_start(out=T_["li"], in_=W["s5_lam_im"][l].rearrange("g p -> p g")))
            k.dma("sp", [], [tb], lambda: nc.sync.dma_start(out=T_["dtv"], in_=W["s5_log_dt"][l].partition_broadcast(64)))
            k.dma("sp", [], [tb], lambda: nc.sync.dma_start(out=T_["Bre"], in_=W["s5_B_re"][l].rearrange("g p h -> p g h")))
            k.dma("sp", [], [tb], lambda: nc.sync.dma_start(out=T_["Bim"], in_=W["s5_B_im"][l].rearrange("g p h -> p g h")))
            k.dma("sp", [], [tb], lambda: nc.sync.dma_start(out=T_["Cld"][:, 0], in_=W["s5_C_re"][l].rearrange("g h p -> (g h) p").rearrange("(c q) p -> q c p", q=128)))
            k.dma("sp", [], [tb], lambda: nc.sync.dma_start(out=T_["Cld"][:, 1], in_=W["s5_C_im"][l].rearrange("g h p -> (g h) p").rearrange("(c q) p -> q c p", q=128)))
            p["Drep"] = sb("Drep%d" % l, [128, 32])
            for s_ in range(8):
                k.dma("sp", [], [pb], lambda s_=s_: nc.sync.dma_start(out=p["Drep"][16 * s_:16 * (s_ + 1), :], in_=W["s5_D"][l].rearrange("(g h) -> h g", h=16)))
            k.op("act", [tb], [tb], lambda: nc.scalar.activation(out=T_["dtv"], in_=T_["dtv"], func=AF.Exp))
            dv(lambda: V.tensor_scalar(out=T_["lr"], in0=T_["lr"], scalar1=-1e-4, scalar2=None, op0=ALU.min))
            dv(lambda: V.tensor_tensor(out=T_["t0"], in0=T_["lr"], in1=T_["dtv"], op=ALU.mult))
            k.op("act", [tb], [tb], lambda: nc.scalar.activation(out=T_["mag"], in_=T_["t0"], func=AF.Exp))
            dv(lambda: V.tensor_tensor(out=T_["ang"], in0=T_["li"], in1=T_["dtv"], op=ALU.mult))
            TWO_PI = 2.0 * np.pi
            for dst, shift in [("sinv", 0.0), ("cosv", np.pi / 2)]:
                dv(lambda shift=shift: V.tensor_scalar(out=T_["t0"], in0=T_["ang"], scalar1=float(shift), scalar2=None, op0=ALU.add))
                dv(lambda: V.tensor_copy(out=T_["t2"], in_=T_["t0"]))
                for m in range(10):
                    thr = float((2 * m + 1) * np.pi)
                    dv(lambda thr=thr: V.tensor_scalar(out=T_["t1"], in0=T_["t0"], scalar1=thr, scalar2=TWO_PI, op0=ALU.is_gt, op1=ALU.mult))
                    dv(lambda: V.tensor_tensor(out=T_["t2"], in0=T_["t2"], in1=T_["t1"], op=ALU.subtract))
                dv(lambda: V.tensor_scalar(out=T_["t2"], in0=T_["t2"], scalar1=float(np.pi), scalar2=float(-np.pi), op0=ALU.min, op1=ALU.max))
                k.op("act", [tb], [tb], lambda dst=dst: nc.scalar.activation(out=T_[dst], in_=T_["t2"], func=AF.Sin))
            dv(lambda: V.tensor_tensor(out=T_["ar"], in0=T_["mag"], in1=T_["cosv"], op=ALU.mult))
            dv(lambda: V.tensor_tensor(out=T_["ai"], in0=T_["mag"], in1=T_["sinv"], op=ALU.mult))
            dv(lambda: V.tensor_scalar(out=T_["nr"], in0=T_["ar"], scalar1=-1.0, scalar2=None, op0=ALU.add))
            dv(lambda: V.tensor_tensor(out=T_["t0"], in0=T_["lr"], in1=T_["lr"], op=ALU.mult))
            dv(lambda: V.tensor_tensor(out=T_["t1"], in0=T_["li"], in1=T_["li"], op=ALU.mult))
            dv(lambda: V.tensor_tensor(out=T_["den"], in0=T_["t0"], in1=T_["t1"], op=ALU.add))
            dv(lambda: V.reciprocal(out=T_["den"], in_=T_["den"]))
            dv(lambda: V.tensor_tensor(out=T_["t0"], in0=T_["nr"], in1=T_["lr"], op=ALU.mult))
            dv(lambda: V.tensor_tensor(out=T_["t1"], in0=T_["ai"], in1=T_["li"], op=ALU.mult))
            dv(lambda: V.tensor_tensor(out=T_["t0"], in0=T_["t0"], in1=T_["t1"], op=ALU.add))
            dv(lambda: V.tensor_tensor(out=T_["cre"], in0=T_["t0"], in1=T_["den"], op=ALU.mult))
            dv(lambda: V.tensor_tensor(out=T_["t0"], in0=T_["ai"], in1=T_["lr"], op=ALU.mult))
            dv(lambda: V.tensor_tensor(out=T_["t1"], in0=T_["nr"], in1=T_["li"], op=ALU.mult))
            dv(lambda: V.tensor_tensor(out=T_["t0"], in0=T_["t0"], in1=T_["t1"], op=ALU.subtract))
            dv(lambda: V.tensor_tensor(out=T_["cim"], in0=T_["t0"], in1=T_["den"], op=ALU.mult))
            cre_b = T_["cre"].unsqueeze(2).to_broadcast([64, 32, 16])
            cim_b = T_["cim"].unsqueeze(2).to_broadcast([64, 32, 16])
            cmul(T_["bbre"], T_["bbim"], cre_b, cim_b, T_["Bre"], T_["Bim"], T_["tA"], T_["tB"])
            for ri, nm in enumerate(["Ccre", "Ccim"]):
                for fc in range(4):
                    ps, pbp = psum("misc")
                    k.op("pe", [tb, cst_b], [pbp, tb], lambda ps=ps, ri=ri, fc=fc: nc.tensor.transpose(ps[0:64, 0:128], T_["Cld"][:, ri, fc, :], ident_f))
                    k.op("dve", [pbp], [tb], lambda ps=ps, nm=nm, fc=fc: V.tensor_copy(
                        out=T_[nm][:, 8 * fc:8 * (fc + 1), :], in_=ps[0:64, 0:128].rearrange("p (g h) -> p g h", g=8)))
            Ap, An = T_["Apos"], T_["Aneg"]
            dv(lambda: V.memset(Ap[:, 0, 0, :], 1.0)); dv(lambda: V.memset(Ap[:, 0, 1, :], 0.0))
            dv(lambda: V.memset(An[:, 0, 0, :], 1.0)); dv(lambda: V.memset(An[:, 0, 1, :], 0.0))
            dv(lambda: V.tensor_copy(out=Ap[:, 1, 0, :], in_=T_["ar"])); dv(lambda: V.tensor_copy(out=Ap[:, 1, 1, :], in_=T_["ai"]))
            for n in range(2, 9):
                cmul(Ap[:, n, 0, :], Ap[:, n, 1, :], Ap[:, n - 1, 0, :], Ap[:, n - 1, 1, :], T_["ar"], T_["ai"], T_["t0"], T_["t1"])
            dv(lambda: V.tensor_tensor(out=T_["t0"], in0=T_["ar"], in1=T_["ar"], op=ALU.mult))
            dv(lambda: V.tensor_tensor(out=T_["t1"], in0=T_["ai"], in1=T_["ai"], op=ALU.mult))
            dv(lambda: V.tensor_tensor(out=T_["t0"], in0=T_["t0"], in1=T_["t1"], op=ALU.add))
            dv(lambda: V.reciprocal(out=T_["t0"], in_=T_["t0"]))
            dv(lambda: V.tensor_tensor(out=An[:, 1, 0, :], in0=T_["ar"], in1=T_["t0"], op=ALU.mult))
            dv(lambda: V.scalar_tensor_tensor(out=An[:, 1, 1, :], in0=T_["ai"], scalar=-1.0, in1=T_["t0"], op0=ALU.mult, op1=ALU.mult))
            for n in range(2, 8):
                cmul(An[:, n, 0, :], An[:, n, 1, :], An[:, n - 1, 0, :], An[:, n - 1, 1, :], An[:, 1, 0, :], An[:, 1, 1, :], T_["t0"], T_["t1"])
            p["AR8"] = sb("AR8_%d" % l, [64, 2, 32]); p["AI8"] = sb("AI8_%d" % l, [64, 2, 32])
            k.op("dve", [tb], [pb], lambda: V.tensor_copy(out=p["AR8"][:, 0, :], in_=Ap[:, 8, 0, :]))
            k.op("dve", [tb], [pb], lambda: V.tensor_copy(out=p["AR8"][:, 1, :], in_=Ap[:, 8, 0, :]))
            k.op("dve", [tb], [pb], lambda: V.tensor_scalar(out=p["AI8"][:, 0, :], in0=Ap[:, 8, 1, :], scalar1=-1.0, scalar2=None, op0=ALU.mult))
            k.op("dve", [tb], [pb], lambda: V.tensor_copy(out=p["AI8"][:, 1, :], in_=Ap[:, 8, 1, :]))
            p["Xc"] = sb("Xc%d" % l, [64, 2, 32]); p["Xc_b"] = Buf("Xc")
            k.op("pool", [], [p["Xc_b"]], lambda: nc.gpsimd.memset(p["Xc"][:], 0.0))
            dbg("s5ar", T_["ar"], [tb])
            dbg("s5ai", T_["ai"], [tb])
            p["Tz_d"] = nc.dram_tensor("Tz_d%d" % l, [128, 32 * 128], BF16, kind="Internal").ap()
            p["G_d"] = nc.dram_tensor("G_d%d" % l, [128, 32 * 2 * 64], BF16, kind="Internal").ap()
            p["H_d"] = nc.dram_tensor("H_d%d" % l, [64, 32 * 2 * 128], BF16, kind="Internal").ap()
            p["s5d_b"] = Buf("s5d")
            p["s5h_b"] = [Buf("s5h0"), Buf("s5h1")]
            for blk in range(16):
                g0 = 2 * blk
                sl = slice(g0, g0 + 2)
                u1 = T_["u1"]; u2 = T_["u2"]
                a_re = An[:, :, 0, sl].rearrange("p s g -> p g s").unsqueeze(3).to_broadcast([64, 2, 8, 16])
                a_im = An[:, :, 1, sl].rearrange("p s g -> p g s").unsqueeze(3).to_broadcast([64, 2, 8, 16])
                b_re = T_["bbre"][:, sl, :].unsqueeze(2).to_broadcast([64, 2, 8, 16])
                b_im = T_["bbim"][:, sl, :].unsqueeze(2).to_broadcast([64, 2, 8, 16])
                cmul(T_["Lre"], T_["Lim"], a_re, a_im, b_re, b_im, u1[:, :, 0:8, :], u2[:, :, 0:8, :])
                a_re = Ap[:, :, 0, sl].rearrange("p s g -> p g s").unsqueeze(3).to_broadcast([64, 2, 9, 16])
                a_im = Ap[:, :, 1, sl].rearrange("p s g -> p g s").unsqueeze(3).to_broadcast([64, 2, 9, 16])
                c_re = T_["Ccre"][:, sl, :].unsqueeze(2).to_broadcast([64, 2, 9, 16])
                c_im = T_["Ccim"][:, sl, :].unsqueeze(2).to_broadcast([64, 2, 9, 16])
                cmul(T_["Rre"], T_["nRim"], a_re, a_im, c_re, c_im, u1, u2, neg_im=True)
                a_re = Ap[:, 7, 0, sl].unsqueeze(2).unsqueeze(3).to_broadcast([64, 2, 8, 16])
                a_im = Ap[:, 7, 1, sl].unsqueeze(2).unsqueeze(3).to_broadcast([64, 2, 8, 16])
                cmul(T_["GTre"], T_["GTim"], a_re, a_im, T_["Lre"], T_["Lim"], u1[:, :, 0:8, :], u2[:, :, 0:8, :])
                for gi in range(2):
                    g = g0 + gi
                    ps, pbp = psum("att")
                    k.grp("pe", [tb], [pbp, tb], [
                        lambda ps=ps, gi=gi: nc.tensor.matmul(ps[:, 0:128], lhsT=T_["Lre"][:, gi].rearrange("p s h -> p (s h)"), rhs=T_["Rre"][:, gi, 0:8, :].rearrange("p s h -> p (s h)"), start=True, stop=False),
                        lambda ps=ps, gi=gi: nc.tensor.matmul(ps[:, 0:128], lhsT=T_["Lim"][:, gi].rearrange("p s h -> p (s h)"), rhs=T_["nRim"][:, gi, 0:8, :].rearrange("p s h -> p (s h)"), start=False, stop=True)])
                    ci = g % 2
                    k.op("dve", [pbp, cst_b], [tzt_b[ci]], lambda ps=ps, ci=ci: V.tensor_tensor(out=mg[ci][:, 0:128], in0=ps[:, 0:128], in1=bmask[:], op=ALU.mult))
                    k.op("dve", [tzt_b[ci], cst_b, pb], [shb], lambda ci=ci, g=g: V.scalar_tensor_tensor(
                        out=Tz_sh[:, g, :], in0=ident_f, scalar=p["Drep"][:, g:g + 1], in1=mg[ci][:, 0:128], op0=ALU.mult, op1=ALU.add))
                ps, pbp = psum("misc")
                k.grp("pe", [tb, cst_b], [pbp, tb], [
                    (lambda ps=ps, gi=gi, ri=ri: nc.tensor.transpose(ps[:, 64 * (2 * gi + ri):64 * (2 * gi + ri + 1)],
                                                                    T_["GTre" if ri == 0 else "GTim"][:, gi].rearrange("p s h -> p (s h)"), ident_f[0:64, 0:64]))
                    for gi in range(2) for ri in range(2)])
                k.op("dve", [pbp], [shb], lambda ps=ps, g0=g0: V.tensor_copy(out=G_sh[:, g0:g0 + 2, :, :].rearrange("p g r q -> p (g r q)"), in_=ps[:, 0:256]))
                gl = g0
                hb_ = p["s5h_b"][0]
                k.op("dve", [tb, hb_], [hsh_b], lambda gl=gl: V.tensor_copy(out=H_sh[:, gl:gl + 2, 0, :], in_=T_["Rre"][:, :, 1:9, :].rearrange("p g s h -> p g (s h)")))
                k.op("dve", [tb, hb_], [hsh_b], lambda gl=gl: V.tensor_copy(out=H_sh[:, gl:gl + 2, 1, :], in_=T_["nRim"][:, :, 1:9, :].rearrange("p g s h -> p g (s h)")))
            k.dma("sp", [hsh_b], [p["s5h_b"][0]], lambda: nc.sync.dma_start(out=p["H_d"], in_=H_sh[:].rearrange("p g r q -> p (g r q)")))
            k.dma("sp", [shb], [p["s5d_b"]], lambda: nc.sync.dma_start(out=p["Tz_d"], in_=Tz_sh[:].rearrange("p g q -> p (g q)")))
            k.dma("sp", [shb], [p["s5d_b"]], lambda: nc.sync.dma_start(out=p["G_d"], in_=G_sh[:].rearrange("p g r q -> p (g r q)")))

        P = []
        for l in range(NL):
            p = {}
            pb = Buf("params%d" % l)
            p["b"] = pb
            p["nw"] = sb("nw%d" % l, [128, 8])
            k.dma("sp", [], [pb], lambda l=l, p=p: nc.sync.dma_start(out=p["nw"][:], in_=W["norm_w"][l].rearrange("(k p) -> p k", p=128)))
            p["cw"] = sb("cw%d" % l, [96, 16, 4])
            for j in range(4):
                k.dma("sp", [], [pb], lambda l=l, p=p, j=j: nc.sync.dma_start(out=p["cw"][:, :, j], in_=W["mlstm_conv"][l][j].rearrange("(b p) -> p b", p=96)))
            p["gbias"] = sb("gbias%d" % l, [128, NCH, 8])
            for c in range(NCH):
                k.dma("sp", [], [pb], lambda l=l, p=p, c=c: nc.sync.dma_start(out=p["gbias"][:, c, :], in_=W["mlstm_gate_b"][l].rearrange("a b -> (a b)").partition_broadcast(128)))
            if l == 0:
                nw_sh = (sb("nwa_sh", [128, 768]), sb("nwb_sh", [128, 768]), Buf("nwsh"))
            p["nwa"], p["nwb"], p["nwsh_b"] = nw_sh
            p["C32"] = sb("C32_%d" % l, [96, 8, 193])
            p["Cbf"] = sb("Cbf_%d" % l, [96, 8, 193], BF16)
            p["C_b"] = [Buf("C%d_%d" % (l, i)) for i in range(8)]
            for i in range(8):
                k.op("pool", [], [p["C_b"][i]], lambda p=p, i=i: nc.gpsimd.memset(p["C32"][:, i, :], 0.0))
                k.op("pool", [], [p["C_b"][i]], lambda p=p, i=i: nc.gpsimd.memset(p["Cbf"][:, i, :], 0.0))
            p["wal"] = sb("wal%d" % l, [16, 384], BF16)
            k.dma("pool", [], [pb], lambda l=l, p=p: nc.gpsimd.dma_start(out=p["wal"][:], in_=W["gla_w_alpha"][l]))
            p["bal"] = sb("bal%d" % l, [1, 384], BF16)
            k.dma("pool", [], [pb], lambda l=l, p=p: nc.gpsimd.dma_start(out=p["bal"][:], in_=W["gla_b_alpha"][l].rearrange("(a n) -> a n", a=1)))
            p["S32"] = sb("S32_%d" % l, [96, 4, 192])
            p["Sbf"] = sb("Sbf_%d" % l, [96, 4, 192], BF16)
            p["S_b"] = [Buf("S%d_%d" % (l, i)) for i in range(4)]
            for i in range(4):
                k.op("pool", [], [p["S_b"][i]], lambda p=p, i=i: nc.gpsimd.memset(p["S32"][:, i, :], 0.0))
                k.op("pool", [], [p["S_b"][i]], lambda p=p, i=i: nc.gpsimd.memset(p["Sbf"][:, i, :], 0.0))
            if stage >= 3:
                s5_setup(l, p)
            p["halo"] = sb("halo%d" % l, [96, 16, 3])
            p["halo_b"] = [Buf("halo") for i in range(16)]
            for i in range(16):
                k.op("pool", [], [p["halo_b"][i]], lambda p=p, i=i: nc.gpsimd.memset(p["halo"][:, i, :], 0.0))
            P.append(p)

        hn = sb("hn", [128, 8, TT], BF16); hn_b = Buf("hn")
        sq = [sb("sq%d" % i, [128, TT], BF16) for i in range(2)]; sq_b = [Buf("sq") for i in range(2)]
        rstd = sb("rstd", [128, TT]); rstd_b = Buf("rstd")
        qT = sb("qT", [96, 8, TT], BF16); qT_b = [Buf("qT%d" % i) for i in range(8)]
        kT = sb("kT", [96, 8, TT], BF16); kT_b = [Buf("kT%d" % i) for i in range(8)]
        cs = [sb("cs%d" % i, [96, TT + 3]) for i in range(2)]; cs_b = [Buf("cs") for i in range(2)]; csh_b = [Buf("csh") for i in range(2)]
        acc = [sb("acc%d" % i, [96, TT]) for i in range(2)]; acc_b = [Buf("acc") for i in range(2)]
        g_sb = sb("g_sb", [128, NCH, 8]); g_b = Buf("g")
        lf = sb("lf", [128, NCH, 4]); lf_b = Buf("lf")
        a_sb = sb("a_sb", [128, NCH * 4]); e_sb = sb("e_sb", [128, NCH * 4]); ir_sb = sb("ir_sb", [128, NCH * 4]); dec_sb = sb("dec_sb", [128, NCH * 4])
        a_b, e_b, r_b, dec_b = Buf("a"), Buf("e"), Buf("r"), Buf("dec")
        v2 = sb("v2", [128, NCH, 4, 193], BF16); v2_b = [Buf("v2_%d" % c) for c in range(NCH)]
        vtmp = sb("vtmp", [128, 4, 193]); vtmp_b = Buf("vtmp")
        k.op("pool", [], [vtmp_b], lambda: nc.gpsimd.memset(vtmp[:, :, 192:193], SA))
        ktok = sb("ktok", [128, NCH, 768], BF16); ktok_b = [Buf("ktok%d" % c) for c in range(NCH)]
        gsa = sb("gsa", [128, NCH, 768], BF16); gsa_b = [Buf("gsa%d" % c) for c in range(NCH)]
        gsb = gsa; gsb_b = gsa_b
        gt = [sb("gt%d" % i, [128, 384], BF16) for i in range(2)]; gt_b = [Buf("gt") for i in range(2)]
        stm = [sb("stm%d" % i, [128, 4, 128], BF16) for i in range(2)]; stm_b = [Buf("stm") for i in range(2)]
        sm = [sb("sm%d" % i, [128, 48]) for i in range(2)]; sm_b = [Buf("sm") for i in range(2)]
        ctmp = [sb("ctmp%d" % i, [96, 386]) for i in range(2)]; ctmp_b = [Buf("ctmp") for i in range(2)]
        ytmp = [sb("ytmp%d" % i, [128, 192]) for i in range(2)]; ytmp_b = [Buf("ytmp") for i in range(2)]
        ya = [sb("ya%d" % i, [128, 768], BF16) for i in range(2)]; ya_b = [Buf("ya") for i in range(2)]
        yaT = sb("yaT", [128, 6, TT], BF16); yaT_b = Buf("yaT")
        baT = sb("baT", [16, TT], BF16); baT_b = Buf("baT")
        la = [sb("la%d" % i, [128, 384]) for i in range(NCH)]; la_b = [Buf("la") for i in range(NCH)]
        eq = sb("eq", [96, 4, TT]); ek = sb("ek", [96, 4, TT]); eqk_b = [Buf("eqk%d" % h) for h in range(4)]
        dgl = sb("dgl", [96, 4, NCH]); dgl_b = [Buf("dgl%d" % h) for h in range(4)]
        qg = qT; qg_b = qT_b[0:4]
        kg = kT; kg_b = kT_b[0:4]
        vg = sb("vg", [128, NCH, 768], BF16); vg_b = [Buf("vg%d" % c) for c in range(NCH)]
        ktokg = sb("ktokg", [128, NCH, 384], BF16); ktokg_b = [Buf("ktokg%d" % c) for c in range(NCH)]
        ybT = sb("ybT", [128, 6, TT], BF16); ybT_b = Buf("ybT")
        Ut = s5w[0:32, 0:2048].bitcast(BF16).rearrange("p (g s h) -> p g s h", g=32, s=8); Ut_b = Buf("Ut")
        Yt = s5w[0:32, 0:2048].bitcast(BF16).rearrange("p (t g h) -> p t g h", t=8, g=32); Yt_b = Ut_b
        Xall = s5w[0:64, 2048:2048 + (JJ + 1) * 64].rearrange("p (j r g) -> p j r g", r=2, g=32); Xall_b = Buf("Xall")
        U_all = sb("U_all", [128, 32, JJ], BF16); U_b = Buf("U_all")
        xbf = sb("xbf", [64, 2, 32, JJ], BF16); xbf_b = Buf("xbf")
        st1 = sb("st1", [64, 2, 32]); st2 = sb("st2", [64, 2, 32]); st_b = Buf("st"); st1_b = Buf("st1"); st2a_b = Buf("st2a"); st2b_b = Buf("st2b")
        yc0 = sb("yc0", [128, 4, TT]); yc0b = sb("yc0b", [128, 4, TT], BF16); yc0_b = [Buf("yc0_%d" % i) for i in range(4)]
        czs = sb("czs", [128, 4, TT], BF16); czs_b = [Buf("czs%d" % i) for i in range(4)]
        ycT = sb("ycT", [128, 4, TT], BF16); ycT_b = [Buf("ycT%d" % i) for i in range(4)]
        sg = [sb("sg%d" % i, [128, 3, TT], BF16) for i in range(2)]; sg_b = [Buf("sg") for i in range(2)]
        mT = sb("mT", [128, 8, TT], BF16); mT_b = [Buf("mT%d" % i) for i in range(8)]
        fnw = sb("fnw", [128, 8]); fnw_b = Buf("fnw")
        k.dma("sp", [], [fnw_b], lambda: nc.sync.dma_start(out=fnw[:], in_=W["final_norm"].rearrange("(k p) -> p k", p=128)))
        o_sb = [sb("o_sb%d" % i, [128, TT]) for i in range(1)]; o_b = [Buf("o") for i in range(1)]
        rr = {"sg": 0, "mg": 0, "sq": 0, "cs": 0, "gt": 0, "stm": 0, "sm": 0, "ctmp": 0, "ytmp": 0, "ya": 0}

        def rot(name, n=2):
            i = rr[name] % n
            rr[name] += 1
            return i

        def proj_fm(wt, wb, ncols, col_off, M, cls="proj"):
            ps, pb = psum(cls)
            wv = wt[:, 0:8 * ncols].rearrange("p (k n) -> p k n", k=8)
            k.grp("pe", [wb, hn_b], [pb], [
                (lambda kk=kk: nc.tensor.matmul(ps[0:M, 0:TT], lhsT=wv[:, kk, col_off:col_off + M], rhs=hn[:, kk, :], start=(kk == 0), stop=(kk == 7)))
                for kk in range(8)])
            return ps, pb

        def proj_tm(wt, wb, ncols, col_off, N, c, cls="proj", ps_pb=None, pcol=0):
            ps, pb = ps_pb if ps_pb is not None else psum(cls)
            wv = wt[:, 0:8 * ncols].rearrange("p (k n) -> p k n", k=8)
            k.grp("pe", [wb, hn_b], [pb], [
                (lambda kk=kk: nc.tensor.matmul(ps[:, pcol:pcol + N], lhsT=hn[:, kk, c * 128:(c + 1) * 128], rhs=wv[:, kk, col_off:col_off + N], start=(kk == 0), stop=(kk == 7)))
                for kk in range(8)])
            return ps, pb

        def norm_stage(l, nw_ap, pbuf):
            ps, pb = psum("misc")
            for kk in range(8):
                i = kk % 2
                k.op("act", [x_b], [sq_b[i]], lambda kk=kk, i=i: nc.scalar.activation(out=sq[i][:], in_=x_sb[:, kk, :], func=AF.Square))
                k.op("pe", [sq_b[i], cstb_b], [pb], lambda kk=kk, i=i: nc.tensor.matmul(ps[:, 0:TT], lhsT=cstb[:, 2, :], rhs=sq[i][:], start=(kk == 0), stop=(kk == 7)))
            k.op("act", [pb], [rstd_b], lambda: nc.scalar.activation(out=rstd[:], in_=ps[:, 0:TT], func=AF.Sqrt, scale=1.0 / D, bias=EPS))
            k.op("dve", [rstd_b], [rstd_b], lambda: nc.vector.reciprocal(out=rstd[:], in_=rstd[:]))

        def chain2(g1, g2):
            for _ in g1:
                yield
            for _ in g2:
                yield

        def interleave(ga, gb, ratio):
            da = db = False
            while not (da and db):
                if not da:
                    try:
                        next(ga)
                    except StopIteration:
                        da = True
                for _ in range(ratio):
                    if not db:
                        try:
                            next(gb)
                        except StopIteration:
                            db = True

        def layer(l, t):
            p = P[l]
            k.mark("L%d_t%d_start" % (l, t))
            k.dma("sp", [], [p["nwsh_b"]], lambda: nc.sync.dma_start(out=p["nwa"][:], in_=W["mlstm_norm"][l].partition_broadcast(128)))
            k.dma("sp", [], [p["nwsh_b"]], lambda: nc.sync.dma_start(out=p["nwb"][:], in_=W["gla_norm"][l].partition_broadcast(128)))
            if stage >= 3:
                k.dma("sp", [p["s5d_b"]], [shb], lambda: nc.sync.dma_start(out=Tz_sh[:].rearrange("p g q -> p (g q)"), in_=p["Tz_d"]))
                k.dma("sp", [p["s5d_b"]], [shb], lambda: nc.sync.dma_start(out=G_sh[:].rearrange("p g r q -> p (g r q)"), in_=p["G_d"]))
                k.dma("sp", [p["s5h_b"][0]], [hsh_b], lambda: nc.sync.dma_start(out=H_sh[:].rearrange("p g r q -> p (g r q)"), in_=p["H_d"]))
            norm_stage(l, p["nw"], p["b"])
            for kk in range(8):
                k.op("dve", [x_b, rstd_b, p["b"]], [hn_b], lambda kk=kk: nc.vector.scalar_tensor_tensor(
                    out=hn[:, kk, :], in0=x_sb[:, kk, :], scalar=p["nw"][:, kk:kk + 1], in1=rstd[:], op0=ALU.mult, op1=ALU.mult), relax=True)
            if stage < 1:
                return
            for qi, (nm, dstT, dst_b) in enumerate([("aq", qT, qT_b), ("ak", kT, kT_b)]):
                for h2 in range(2):
                    wt, wb = wload(l, "%s%d" % (nm, h2))
                    def _cstream(js, ci, qi=qi, h2=h2, wt=wt, wb=wb, dstT=dstT, dst_b=dst_b):
                        for j in js:
                            blk = 4 * h2 + j
                            gb = 8 * qi + blk
                            ps, pb = proj_fm(wt, wb, 384, 96 * j, 96)
                            hb = p["halo_b"][gb]
                            k.op("act", [hb], [csh_b[ci]], lambda ci=ci, gb=gb: nc.scalar.copy(out=cs[ci][:, 0:3], in_=p["halo"][:, gb, :]))
                            k.op("act", [pb], [cs_b[ci]], lambda ci=ci, ps=ps: nc.scalar.copy(out=cs[ci][:, 3:TT + 3], in_=ps[0:96, 0:TT]))
                            k.op("act", [pb], [hb], lambda ps=ps, gb=gb: nc.scalar.copy(out=p["halo"][:, gb, :], in_=ps[0:96, TT - 3:TT]))
                            yield
                            k.op("dve", [cs_b[ci], csh_b[ci], p["b"]], [acc_b[ci]], lambda ci=ci, gb=gb: nc.vector.tensor_scalar(
                                out=acc[ci][:], in0=cs[ci][:, 0:TT], scalar1=p["cw"][:, gb, 0:1], scalar2=None, op0=ALU.mult))
                            yield
                            for jj in range(1, 4):
                                k.op("dve", [cs_b[ci], csh_b[ci], p["b"], acc_b[ci]], [acc_b[ci]], lambda ci=ci, gb=gb, jj=jj: nc.vector.scalar_tensor_tensor(
                                    out=acc[ci][:], in0=cs[ci][:, jj:jj + TT], scalar=p["cw"][:, gb, jj:jj + 1], in1=acc[ci][:], op0=ALU.mult, op1=ALU.add))
                                yield
                            k.op("act", [acc_b[ci]], [dst_b[blk]], lambda ci=ci, blk=blk, dstT=dstT: nc.scalar.activation(out=dstT[:, blk, :], in_=acc[ci][:], func=AF.Silu))
                    interleave(_cstream([0, 2], 0), _cstream([1, 3], 1), 1)
            dbg("qT", qT[:], qT_b)
            dbg("kT", kT[:], kT_b)
            k.mark("L%d_t%d_qkdone" % (l, t))
            wt, wb = wload(l, "aif")
            psg, pbg = psum("misc")
            for c in range(NCH):
                proj_tm(wt, wb, 8, 0, 8, c, ps_pb=(psg, pbg), pcol=8 * c)
            k.op("dve", [pbg, p["b"]], [g_b], lambda: nc.vector.tensor_tensor(
                out=g_sb[:].rearrange("p c e -> p (c e)"), in0=psg[:, 0:8 * NCH], in1=p["gbias"][:].rearrange("p c e -> p (c e)"), op=ALU.add))
            k.op("act", [g_b], [lf_b], lambda: nc.scalar.activation(out=lf[:], in_=g_sb[:, :, 4:8], func=AF.Sigmoid))
            k.op("act", [lf_b], [lf_b], lambda: nc.scalar.activation(out=lf[:], in_=lf[:], func=AF.Ln))
            lf2 = lf[:].rearrange("p c e -> p (c e)")
            for h2 in range(2):
                wto, wbo = wload(l, "ao%d" % h2)
                wtz, wbz = wload(l, "az%d" % h2)
                for c in range(NCH):
                    ps, pb = proj_tm(wto, wbo, 384, 0, 384, c)
                    i0 = rot("gt")
                    k.op("act", [pb], [gt_b[i0]], lambda ps=ps, i0=i0: nc.scalar.activation(out=gt[i0][:], in_=ps[:, 0:384], func=AF.Sigmoid))
                    ps, pb = proj_tm(wtz, wbz, 384, 0, 384, c)
                    i1 = rot("gt")
                    k.op("act", [pb], [gt_b[i1]], lambda ps=ps, i1=i1: nc.scalar.activation(out=gt[i1][:], in_=ps[:, 0:384], func=AF.Silu))
                    k.op("pool", [gt_b[i0], gt_b[i1]], [gsa_b[c]], lambda c=c, h2=h2, i0=i0, i1=i1: nc.gpsimd.tensor_tensor(
                        out=gsa[:, c, 384 * h2:384 * (h2 + 1)], in0=gt[i0][:], in1=gt[i1][:], op=ALU.mult))
            psb, pbb = psum("misc")
            k.op("pe", [lf_b, cst_b], [pbb], lambda: nc.tensor.matmul(psb[:, 0:4 * NCH], lhsT=tri_f, rhs=lf2, start=True, stop=True))
            k.op("pe", [lf_b, cst_b], [pbb], lambda: nc.tensor.matmul(psb[:, 64:64 + 4 * NCH], lhsT=ones_f, rhs=lf2, start=True, stop=True))
            k.op("dve", [g_b, pbb], [a_b], lambda: nc.vector.tensor_tensor(
                out=a_sb[:].rearrange("p (c e) -> p c e", c=NCH), in0=g_sb[:, :, 0:4], in1=psb[:, 0:4 * NCH].rearrange("p (c e) -> p c e", c=NCH), op=ALU.subtract))
            k.op("act", [a_b], [e_b], lambda: nc.scalar.activation(out=e_sb[:], in_=a_sb[:], func=AF.Exp))
            k.op("act", [pbb], [r_b], lambda: nc.scalar.activation(out=ir_sb[:], in_=psb[:, 0:4 * NCH], func=AF.Exp, scale=-1.0))
            k.op("act", [pbb], [dec_b], lambda: nc.scalar.activation(out=dec_sb[:], in_=psb[:, 64:64 + 4 * NCH], func=AF.Exp))
            for h2 in range(2):
                wt, wb = wload(l, "av%d" % h2)
                for c in range(NCH):
                    ps, pb = proj_tm(wt, wb, 384, 0, 384, c)
                    k.op("act", [pb], [vtmp_b], lambda ps=ps, h2=h2: nc.scalar.activation(
                        out=vtmp[:, 2 * h2:2 * h2 + 2, 0:192], in_=ps[:, 0:384].rearrange("p (h e) -> p h e", h=2), func=AF.Copy, scale=SA))
                    for hh in range(2):
                        h = 2 * h2 + hh
                        k.op("dve", [vtmp_b, e_b], [v2_b[c]], lambda c=c, h=h: nc.vector.tensor_scalar(
                            out=v2[:, c, h, :], in0=vtmp[:, h, :], scalar1=e_sb[:, 4 * c + h:4 * c + h + 1], scalar2=None, op0=ALU.mult))
            for c in range(NCH):
                ps, pb = psum("misc")
                psv = ps[:].bitcast(BF16)
                k.grp("pe", kT_b + [cstb_b], [pb], [
                    (lambda blk=blk: nc.tensor.transpose(psv[:, 96 * blk:96 * (blk + 1)], kT[:, blk, c * 128:(c + 1) * 128], ident_b[0:96, 0:96]))
                    for blk in range(8)])
                k.op("act", [pb], [ktok_b[c]], lambda c=c, psv=psv: nc.scalar.copy(out=ktok[:, c, :], in_=psv[:, 0:768]))
            k.mark("L%d_t%d_mloop" % (l, t))
            def _gla_pre():
                k.mark("L%d_t%d_gla" % (l, t))
                wt, wb = wload(l, "ba")
                ps, pb = proj_fm(wt, wb, 16, 0, 16)
                k.op("act", [pb], [baT_b], lambda ps=ps: nc.scalar.copy(out=baT[:], in_=ps[0:16, 0:TT]))
                yield
                for c in range(NCH):
                    ps, pb = psum("proj")
                    k.grp("pe", [baT_b, p["b"], cstb_b], [pb], [
                        lambda ps=ps, c=c: nc.tensor.matmul(ps[:, 0:384], lhsT=baT[:, c * 128:(c + 1) * 128], rhs=p["wal"][:], start=True, stop=False),
                        lambda ps=ps: nc.tensor.matmul(ps[:, 0:384], lhsT=cstb[0:1, 2, :], rhs=p["bal"][:], start=False, stop=True)])
                    yield
                    k.op("act", [pb], [la_b[c]], lambda ps=ps, c=c: nc.scalar.activation(out=la[c][:], in_=ps[:, 0:384], func=AF.Sigmoid))
                    yield
                    k.op("act", [la_b[c]], [la_b[c]], lambda c=c: nc.scalar.activation(out=la[c][:], in_=la[c][:], func=AF.Ln))
                    yield
                for h in range(4):
                    ps, pb = psum("proj")
                    for c in range(NCH):
                        k.op("pe", [la_b[c], cst_b], [pb], lambda ps=ps, c=c, h=h: nc.tensor.matmul(
                            ps[0:96, c * 128:(c + 1) * 128], lhsT=la[c][:, 96 * h:96 * (h + 1)], rhs=tri_f, start=True, stop=True))
                        yield
                    k.op("act", [pb], [eqk_b[h]], lambda ps=ps, h=h: nc.scalar.activation(out=eq[:, h, :], in_=ps[0:96, 0:TT], func=AF.Exp, scale=1.0 / 16))
                    yield
                    k.op("act", [pb], [eqk_b[h]], lambda ps=ps, h=h: nc.scalar.activation(out=ek[:, h, :], in_=ps[0:96, 0:TT], func=AF.Exp, scale=-1.0 / 16))
                    yield
                    k.op("pool", [eqk_b[h]], [dgl_b[h]], lambda h=h: nc.gpsimd.tensor_copy(
                        out=dgl[:, h, :], in_=eq[:, h, :].rearrange("p (c t) -> p c t", c=NCH)[:, :, 127]))
                    yield
                for nm, dst, dst_b, ee, scl in [("bq", qg, qg_b, eq, SB_), ("bk", kg, kg_b, ek, 1.0)]:
                    wt, wb = wload(l, nm)
                    for h in range(4):
                        ps, pb = proj_fm(wt, wb, 384, 96 * h, 96)
                        k.op("dve", [pb, eqk_b[h]], [dst_b[h]], lambda ps=ps, h=h, dst=dst, ee=ee, scl=scl: nc.vector.scalar_tensor_tensor(
                            out=dst[:, h, :], in0=ps[0:96, 0:TT], scalar=scl, in1=ee[:, h, :], op0=ALU.mult, op1=ALU.mult))
                        yield
                for h2 in range(2):
                    wt, wb = wload(l, "bv%d" % h2)
                    for c in range(NCH):
                        ps, pb = proj_tm(wt, wb, 384, 0, 384, c)
                        k.op("act", [pb], [vg_b[c]], lambda ps=ps, c=c, h2=h2: nc.scalar.copy(out=vg[:, c, 384 * h2:384 * (h2 + 1)], in_=ps[:, 0:384]))
                        yield
                for h2 in range(2):
                    wt, wb = wload(l, "bz%d" % h2)
                    for c in range(NCH):
                        ps, pb = proj_tm(wt, wb, 384, 0, 384, c)
                        k.op("act", [pb], [gsb_b[c]], lambda ps=ps, c=c, h2=h2: nc.scalar.activation(out=gsb[:, c, 384 * h2:384 * (h2 + 1)], in_=ps[:, 0:384], func=AF.Silu))
                        yield
                for c in range(NCH):
                    ps, pb = psum("proj")
                    psv = ps[:].bitcast(BF16)
                    k.grp("pe", kg_b + [cstb_b], [pb], [
                        (lambda h=h, c=c, psv=psv: nc.tensor.transpose(psv[:, 96 * h:96 * (h + 1)], kg[:, h, c * 128:(c + 1) * 128], ident_b[0:96, 0:96]))
                        for h in range(4)])
                    yield
                    k.op("act", [pb], [ktokg_b[c]], lambda c=c, psv=psv: nc.scalar.copy(out=ktokg[:, c, :], in_=psv[:, 0:384]))
                    yield

            def _mloop():
                V = nc.vector
                tri4 = tri_f.unsqueeze(1).to_broadcast([128, 4, 128])
                for c in range(NCH):
                    tsl = slice(c * 128, (c + 1) * 128)
                    yi = rot("ya")
                    pss, pbs = psum("att")
                    k.grp("pe", kT_b + qT_b, [pbs], [
                        (lambda j=j, h=h, pss=pss: nc.tensor.matmul(pss[:, 128 * h:128 * (h + 1)], lhsT=kT[:, 2 * h + j, tsl], rhs=qT[:, 2 * h + j, tsl], start=(j == 0), stop=(j == 1)))
                        for h in range(4) for j in range(2)])
                    si = rot("stm")
                    k.op("dve", [pbs, cst_b], [stm_b[si]], lambda si=si, pss=pss: V.tensor_tensor(
                        out=stm[si][:], in0=pss[:, 0:512].rearrange("p (h t) -> p h t", h=4), in1=tri4, op=ALU.mult))
                    yield
                    pair = []
                    for hp in range(2):
                        pso, pbo = psum("pair")
                        fns = []
                        for hh in range(2):
                            h = 2 * hp + hh
                            oc = 256 * hh
                            fns.append(lambda si=si, pso=pso, h=h, oc=oc: nc.tensor.matmul(pso[:, oc:oc + 193], lhsT=stm[si][:, h, :], rhs=v2[:, c, h, :], start=True, stop=False))
                            for j in range(2):
                                fns.append(lambda j=j, pso=pso, h=h, oc=oc: nc.tensor.matmul(pso[:, oc:oc + 193], lhsT=qT[:, 2 * h + j, tsl], rhs=p["Cbf"][:, 2 * h + j, :], start=False, stop=(j == 1)))
                        k.grp("pe", [stm_b[si], v2_b[c]] + qT_b[4 * hp:4 * hp + 4] + p["C_b"][4 * hp:4 * hp + 4], [pbo], fns)
                        pair.append((pso, pbo))
                    yield
                    for h in range(4):
                        psd, pbd = psum("att")
                        k.grp("pe", [ktok_b[c], v2_b[c]], [pbd], [
                            (lambda j=j, psd=psd, h=h: nc.tensor.matmul(psd[0:96, 193 * j:193 * (j + 1)], lhsT=ktok[:, c, 96 * (2 * h + j):96 * (2 * h + j + 1)], rhs=v2[:, c, h, :], start=True, stop=True))
                            for j in range(2)])
                        ti = rot("ctmp")
                        cbs = [p["C_b"][2 * h], p["C_b"][2 * h + 1]]
                        dcol = dec_sb[0:96, 4 * c + h:4 * c + h + 1]
                        c32v = p["C32"][:, 2 * h:2 * h + 2, :].rearrange("p a e -> p (a e)")
                        cbfv = p["Cbf"][:, 2 * h:2 * h + 2, :].rearrange("p a e -> p (a e)")
                        k.op("dve", [pbd] + cbs, [ctmp_b[ti]], lambda psd=psd, ti=ti, c32v=c32v: V.tensor_tensor(out=ctmp[ti][:], in0=psd[0:96, 0:386], in1=c32v, op=ALU.add))
                        k.op("dve", [ctmp_b[ti], dec_b], cbs, lambda ti=ti, c32v=c32v, dcol=dcol: V.tensor_scalar(out=c32v, in0=ctmp[ti][:], scalar1=dcol, scalar2=None, op0=ALU.mult))
                        k.op("act", [ctmp_b[ti], dec_b], cbs, lambda ti=ti, cbfv=cbfv, dcol=dcol: nc.scalar.activation(out=cbfv, in_=ctmp[ti][:], func=AF.Copy, scale=dcol))
                        if h % 2 == 1:
                            yield
                    for hp in range(2):
                        pso, pbo = pair[hp]
                        mi = rot("sm")
                        S_ = sm[mi]; smb = sm_b[mi]
                        for hh in range(2):
                            k.op("dve", [pbo], [smb], lambda S_=S_, pso=pso, hh=hh: V.bn_stats(out=S_[:, 6 * hh:6 * hh + 6], in_=pso[:, 256 * hh:256 * hh + 192]))
                        for hh in range(2):
                            k.op("dve", [smb], [smb], lambda S_=S_, hh=hh: V.bn_aggr(out=S_[:, 12 + 2 * hh:14 + 2 * hh], in_=S_[:, 6 * hh:6 * hh + 6]))
                        irp = ir_sb[:, 4 * c + 2 * hp:4 * c + 2 * hp + 2]
                        k.op("dve", [pbo, r_b, smb], [smb], lambda S_=S_, pso=pso, irp=irp: V.tensor_tensor(out=S_[:, 16:18], in0=pso[:, 192:449:256], in1=irp, op=ALU.max))
                        k.op("dve", [pbo, smb], [smb], lambda S_=S_, pso=pso: V.scalar_tensor_tensor(out=S_[:, 18:20], in0=pso[:, 192:449:256], scalar=-1.0, in1=S_[:, 16:18], op0=ALU.mult, op1=ALU.max))
                        k.op("dve", [smb], [smb], lambda S_=S_: V.scalar_tensor_tensor(out=S_[:, 20:22], in0=S_[:, 18:20], scalar=EPS, in1=S_[:, 18:20], op0=ALU.mult, op1=ALU.mult))
                        k.op("dve", [smb], [smb], lambda S_=S_: V.tensor_tensor(out=S_[:, 22:24], in0=S_[:, 20:22], in1=S_[:, 13:16:2], op=ALU.add))
                        k.op("act", [smb], [smb], lambda S_=S_: nc.scalar.activation(out=S_[:, 24:26], in_=S_[:, 22:24], func=AF.Sqrt))
                        k.op("dve", [smb], [smb], lambda S_=S_: V.reciprocal(out=S_[:, 26:28], in_=S_[:, 24:26]))
                        for hh in range(2):
                            h = 2 * hp + hh
                            yt = rot("ytmp")
                            k.op("dve", [pbo, smb], [ytmp_b[yt]], lambda S_=S_, pso=pso, yt=yt, hh=hh: V.tensor_scalar(
                                out=ytmp[yt][:], in0=pso[:, 256 * hh:256 * hh + 192], scalar1=S_[:, 12 + 2 * hh:13 + 2 * hh], scalar2=S_[:, 26 + hh:27 + hh], op0=ALU.subtract, op1=ALU.mult))
                            k.op("pool", [ytmp_b[yt], p["nwsh_b"]], [ytmp_b[yt]], lambda yt=yt, h=h: nc.gpsimd.tensor_tensor(
                                out=ytmp[yt][:], in0=ytmp[yt][:], in1=p["nwa"][:, 192 * h:192 * (h + 1)], op=ALU.mult))
                            k.op("pool", [ytmp_b[yt], gsa_b[c]], [ya_b[yi]], lambda yt=yt, h=h, yi=yi: nc.gpsimd.tensor_tensor(
                                out=ya[yi][:, 192 * h:192 * (h + 1)], in0=ytmp[yt][:], in1=gsa[:, c, 192 * h:192 * (h + 1)], op=ALU.mult))
                        yield
                    dbg("ya", ya[yi][:], [ya_b[yi]], view=lambda d_, c=c: d_[c * 128:(c + 1) * 128, :])
                    ps, pb = psum("misc")
                    psv = ps[:].bitcast(BF16)
                    k.grp("pe", [ya_b[yi], cstb_b], [pb], [
                        (lambda j=j, yi=yi, psv=psv: nc.tensor.transpose(psv[:, 128 * j:128 * (j + 1)], ya[yi][:, 128 * j:128 * (j + 1)], ident_b))
                        for j in range(6)])
                    k.op("act", [pb], [yaT_b], lambda c=c, psv=psv: nc.scalar.copy(out=yaT[:, :, c * 128:(c + 1) * 128], in_=psv[:, 0:768].rearrange("p (j t) -> p j t", j=6)))
                    yield
            pmode["wide"] = False
            if stage >= 3:
                interleave(_mloop(), s5_pre(l, t), 2)
            else:
                for _ in _mloop():
                    pass
            pmode["wide"] = True
            if stage >= 2:
                for _ in _gla_pre():
                    pass
            dbg("yaT", yaT[:], [yaT_b])
            if stage < 2:
                return
            k.mark("L%d_t%d_gloop" % (l, t))
            def _gloop():
                V = nc.vector
                tri4 = tri_f.unsqueeze(1).to_broadcast([128, 4, 128])
                for c in range(NCH):
                    tsl = slice(c * 128, (c + 1) * 128)
                    yi = rot("ya")
                    pss, pbs = psum("att")
                    k.grp("pe", kg_b + qg_b, [pbs], [
                        (lambda h=h, pss=pss: nc.tensor.matmul(pss[:, 128 * h:128 * (h + 1)], lhsT=kg[:, h, tsl], rhs=qg[:, h, tsl], start=True, stop=True))
                        for h in range(4)])
                    yield
                    si = rot("stm")
                    k.op("dve", [pbs, cst_b], [stm_b[si]], lambda si=si, pss=pss: V.tensor_tensor(
                        out=stm[si][:], in0=pss[:, 0:512].rearrange("p (h t) -> p h t", h=4), in1=tri4, op=ALU.mult))
                    yield
                    pair = []
                    for hp in range(2):
                        pso, pbo = psum("pair")
                        fns = []
                        for hh in range(2):
                            h = 2 * hp + hh
                            oc = 256 * hh
                            fns.append(lambda si=si, pso=pso, h=h, oc=oc: nc.tensor.matmul(pso[:, oc:oc + 192], lhsT=stm[si][:, h, :], rhs=vg[:, c, 192 * h:192 * (h + 1)], start=True, stop=False))
                            fns.append(lambda pso=pso, h=h, oc=oc: nc.tensor.matmul(pso[:, oc:oc + 192], lhsT=qg[:, h, tsl], rhs=p["Sbf"][:, h, :], start=False, stop=True))
                        k.grp("pe", [stm_b[si], vg_b[c]] + qg_b[2 * hp:2 * hp + 2] + p["S_b"][2 * hp:2 * hp + 2], [pbo], fns)
                        pair.append((pso, pbo))
                        yield
                    for hp in range(2):
                        psd, pbd = psum("att")
                        k.grp("pe", [ktokg_b[c], vg_b[c]], [pbd], [
                            (lambda hh=hh, psd=psd, hp=hp: nc.tensor.matmul(psd[0:96, 192 * hh:192 * (hh + 1)], lhsT=ktokg[:, c, 96 * (2 * hp + hh):96 * (2 * hp + hh + 1)],
                                                                           rhs=vg[:, c, 192 * (2 * hp + hh):192 * (2 * hp + hh + 1)], start=True, stop=True))
                            for hh in range(2)])
                        yield
                        ti = rot("ctmp")
                        sbs = p["S_b"][2 * hp:2 * hp + 2]
                        s32v = p["S32"][:, 2 * hp:2 * hp + 2, :].rearrange("p a e -> p (a e)")
                        k.op("dve", [pbd] + sbs, [ctmp_b[ti]], lambda psd=psd, ti=ti, s32v=s32v: V.tensor_tensor(out=ctmp[ti][:, 0:384], in0=psd[0:96, 0:384], in1=s32v, op=ALU.add))
                        yield
                        for hh in range(2):
                            h = 2 * hp + hh
                            k.op("dve", [ctmp_b[ti], dgl_b[h]], [p["S_b"][h]], lambda ti=ti, h=h, hh=hh, c=c: V.tensor_scalar(
                                out=p["S32"][:, h, :], in0=ctmp[ti][:, 192 * hh:192 * (hh + 1)], scalar1=dgl[:, h, c:c + 1], scalar2=None, op0=ALU.mult))
                            yield
                            k.op("act", [ctmp_b[ti], dgl_b[h]], [p["S_b"][h]], lambda ti=ti, h=h, hh=hh, c=c: nc.scalar.activation(
                                out=p["Sbf"][:, h, :], in_=ctmp[ti][:, 192 * hh:192 * (hh + 1)], func=AF.Copy, scale=dgl[:, h, c:c + 1]))
                            yield
                    mi = rot("sm")
                    S_ = sm[mi]; smb = sm_b[mi]
                    for h in range(4):
                        pso, pbo = pair[h // 2]
                        k.op("dve", [pbo], [smb], lambda S_=S_, pso=pso, h=h: V.bn_stats(out=S_[:, 6 * h:6 * h + 6], in_=pso[:, 256 * (h % 2):256 * (h % 2) + 192]))
                        yield
                    for h in range(4):
                        k.op("dve", [smb], [smb], lambda S_=S_, h=h: V.bn_aggr(out=S_[:, 24 + 2 * h:26 + 2 * h], in_=S_[:, 6 * h:6 * h + 6]))
                        yield
                    k.op("dve", [smb], [smb], lambda S_=S_: V.tensor_tensor(out=S_[:, 32:36], in0=S_[:, 24:32:2], in1=S_[:, 24:32:2], op=ALU.mult))
                    yield
                    k.op("dve", [smb], [smb], lambda S_=S_: V.tensor_tensor(out=S_[:, 36:40], in0=S_[:, 32:36], in1=S_[:, 25:32:2], op=ALU.add))
                    yield
                    k.op("act", [smb], [smb], lambda S_=S_: nc.scalar.activation(out=S_[:, 40:44], in_=S_[:, 36:40], func=AF.Sqrt, bias=EPS, scale=1.0))
                    yield
                    k.op("dve", [smb], [smb], lambda S_=S_: V.reciprocal(out=S_[:, 44:48], in_=S_[:, 40:44]))
                    yield
                    for h in range(4):
                        pso, pbo = pair[h // 2]
                        yt = rot("ytmp")
                        k.op("dve", [pbo, smb, p["nwsh_b"]], [ytmp_b[yt]], lambda S_=S_, pso=pso, yt=yt, h=h: V.scalar_tensor_tensor(
                            out=ytmp[yt][:], in0=pso[:, 256 * (h % 2):256 * (h % 2) + 192], scalar=S_[:, 44 + h:45 + h], in1=p["nwb"][:, 192 * h:192 * (h + 1)], op0=ALU.mult, op1=ALU.mult))
                        yield
                        k.op("pool", [ytmp_b[yt], gsb_b[c]], [ya_b[yi]], lambda yt=yt, h=h, yi=yi, c=c: nc.gpsimd.tensor_tensor(
                            out=ya[yi][:, 192 * h:192 * (h + 1)], in0=ytmp[yt][:], in1=gsb[:, c, 192 * h:192 * (h + 1)], op=ALU.mult))
                        yield
                    dbg("yb", ya[yi][:], [ya_b[yi]], view=lambda d_, c=c: d_[c * 128:(c + 1) * 128, :])
                    ps, pb = psum("misc")
                    psv = ps[:].bitcast(BF16)
                    k.grp("pe", [ya_b[yi], cstb_b], [pb], [
                        (lambda j=j, yi=yi, psv=psv: nc.tensor.transpose(psv[:, 128 * j:128 * (j + 1)], ya[yi][:, 128 * j:128 * (j + 1)], ident_b))
                        for j in range(6)])
                    k.op("act", [pb], [ybT_b], lambda c=c, psv=psv: nc.scalar.copy(out=ybT[:, :, c * 128:(c + 1) * 128], in_=psv[:, 0:768].rearrange("p (j t) -> p j t", j=6)))
                    yield
            pmode["wide"] = False
            if stage >= 3:
                interleave(_gloop(), s5_scan(l, t), 2)
            else:
                for _ in _gloop():
                    pass
            pmode["wide"] = True
            if stage < 3:
                return
            k.mark("L%d_t%d_s5" % (l, t))
            s5_stage(l, t)
            if stage < 4:
                return
            k.mark("L%d_t%d_merge" % (l, t))
            merge_stage(l, t)
            k.mark("L%d_t%d_end" % (l, t))

        def s5_pre(l, t):
            p = P[l]
            tb = s5t["b"]
            V = nc.vector
            for fc in range(4):
                if fc % 2 == 0:
                    wt, wb = wload(l, "cz%d" % (fc // 2))
                ps, pb = proj_fm(wt, wb, 256, 128 * (fc % 2), 128)
                k.op("act", [pb], [czs_b[fc]], lambda ps=ps, fc=fc: nc.scalar.activation(out=czs[:, fc, :], in_=ps[:, 0:TT], func=AF.Silu))
                yield
            k.mark("L%d_t%d_s5a_czdone" % (l, t))
            for half in range(2):
                wt, wb = wload(l, "cu%d" % half)
                wv = wt[:, 0:8 * 256].rearrange("p (k n) -> p k n", k=8)
                for sp_ in range(4):
                    ps, pb = psum("proj")
                    fns = []
                    for s2 in range(2):
                        s_ = 2 * sp_ + s2
                        for kk in range(8):
                            fns.append(lambda ps=ps, s_=s_, s2=s2, kk=kk, wv=wv: nc.tensor.matmul(
                                ps[0:JJ, 256 * s2:256 * (s2 + 1)], lhsT=hn[:, kk, s_::8], rhs=wv[:, kk, :], start=(kk == 0), stop=(kk == 7)))
                    k.grp("pe", [wb, hn_b], [pb], fns)
                    k.op("act", [pb, tb], [Ut_b], lambda ps=ps, sp_=sp_, half=half: nc.scalar.copy(
                        out=Ut[:, 16 * half:16 * half + 16, 2 * sp_:2 * sp_ + 2, :], in_=ps[0:JJ, 0:512].rearrange("p (s g h) -> p g s h", s=2, g=16)))
                    yield
            k.mark("L%d_t%d_s5b_utdone" % (l, t))
            ps, pb = psum("misc")
            psv = ps[:].bitcast(BF16)
            k.grp("pe", [Ut_b, cstb_b], [pb], [
                (lambda g=g, psv=psv: nc.tensor.transpose(psv[:, JJ * g:JJ * (g + 1)], Ut[:, g, :, :], ident_b[0:JJ, 0:JJ]))
                for g in range(32)])
            k.op("act", [pb], [U_b], lambda psv=psv: nc.scalar.copy(out=U_all[:, 0:16, :].rearrange("p g j -> p (g j)"), in_=psv[:, 0:16 * JJ]))
            k.op("dve", [pb], [U_b], lambda psv=psv: V.tensor_copy(out=U_all[:, 16:32, :].rearrange("p g j -> p (g j)"), in_=psv[:, 16 * JJ:32 * JJ]))
            yield
            k.mark("L%d_t%d_s5c_ualldone" % (l, t))
            k.op("dve", [p["Xc_b"], tb], [Xall_b], lambda: V.tensor_copy(out=Xall[:, 0, :, :], in_=p["Xc"][:]))
            for ri in range(2):
                for gh in range(2):
                    ps, pb = psum("proj")
                    k.grp("pe", [shb, U_b], [pb], [
                        (lambda ps=ps, ri=ri, g=g, gi=gi: nc.tensor.matmul(ps[0:64, JJ * gi:JJ * (gi + 1)], lhsT=G_sh[:, g, ri, :], rhs=U_all[:, g, :], start=True, stop=True))
                        for gi, g in enumerate(range(16 * gh, 16 * gh + 16))])
                    k.op("act", [pb, tb], [Xall_b], lambda ps=ps, ri=ri, gh=gh: nc.scalar.copy(
                        out=Xall[:, 1:JJ + 1, ri, 16 * gh:16 * gh + 16], in_=ps[0:64, 0:16 * JJ].rearrange("p (g j) -> p j g", g=16)))
                    yield
            yield

        def s5_scan(l, t):
            p = P[l]
            tb = s5t["b"]
            V = nc.vector
            k.mark("L%d_t%d_s5d_wdone" % (l, t))
            rd = [Xall_b, p["b"]]
            for j in range(JJ):
                k.op("dve", rd, [st1_b], lambda j=j: V.tensor_tensor(out=st1[:], in0=Xall[:, j, :, :], in1=p["AR8"][:], op=ALU.mult))
                yield
                k.op("dve", rd, [st2a_b], lambda j=j: V.tensor_tensor(out=st2[:, 0, :], in0=Xall[:, j, 1, :], in1=p["AI8"][:, 0, :], op=ALU.mult))
                yield
                k.op("dve", rd, [st2b_b], lambda j=j: V.tensor_tensor(out=st2[:, 1, :], in0=Xall[:, j, 0, :], in1=p["AI8"][:, 1, :], op=ALU.mult))
                yield
                k.op("dve", [Xall_b, st1_b], [Xall_b], lambda j=j: V.tensor_tensor(out=Xall[:, j + 1, :, :], in0=Xall[:, j + 1, :, :], in1=st1[:], op=ALU.add))
                yield
                k.op("dve", [Xall_b, st2a_b, st2b_b], [Xall_b], lambda j=j: V.tensor_tensor(out=Xall[:, j + 1, :, :], in0=Xall[:, j + 1, :, :], in1=st2[:], op=ALU.add))
                yield
            k.op("dve", [Xall_b], [p["Xc_b"]], lambda: V.tensor_copy(out=p["Xc"][:], in_=Xall[:, JJ, :, :]))
            for ri in range(2):
                k.op("act", [Xall_b], [xbf_b], lambda ri=ri: nc.scalar.copy(out=xbf[:, ri, :, :], in_=Xall[:, 0:JJ, ri, :].rearrange("p j g -> p g j")))
            k.mark("L%d_t%d_s5e_scandone" % (l, t))
            yield

        def s5_stage(l, t):
            p = P[l]
            tb = s5t["b"]
            V = nc.vector
            for blk in range(8):
                ps, pb = psum("att")
                fns = []
                for gi in range(4):
                    g = 4 * blk + gi
                    fns.append(lambda ps=ps, g=g, gi=gi: nc.tensor.matmul(ps[0:JJ, 128 * gi:128 * (gi + 1)], lhsT=U_all[:, g, :], rhs=Tz_sh[:, g, :], start=True, stop=False))
                    fns.append(lambda ps=ps, g=g, gi=gi: nc.tensor.matmul(ps[0:JJ, 128 * gi:128 * (gi + 1)], lhsT=xbf[:, 0, g, :], rhs=H_sh[:, g, 0, :], start=False, stop=False))
                    fns.append(lambda ps=ps, g=g, gi=gi: nc.tensor.matmul(ps[0:JJ, 128 * gi:128 * (gi + 1)], lhsT=xbf[:, 1, g, :], rhs=H_sh[:, g, 1, :], start=False, stop=True))
                k.grp("pe", [U_b, shb, hsh_b, xbf_b], [pb], fns)
                eng = "act" if blk % 2 == 0 else "dve"
                if eng == "act":
                    k.op("act", [pb, tb], [Yt_b], lambda ps=ps, blk=blk: nc.scalar.copy(
                        out=Yt[:, :, 4 * blk:4 * blk + 4, :], in_=ps[0:JJ, 0:512].rearrange("p (g t h) -> p t g h", g=4, t=8)))
                else:
                    k.op("dve", [pb, tb], [Yt_b], lambda ps=ps, blk=blk: V.tensor_copy(
                        out=Yt[:, :, 4 * blk:4 * blk + 4, :], in_=ps[0:JJ, 0:512].rearrange("p (g t h) -> p t g h", g=4, t=8)))
            k.mark("L%d_t%d_s5f_ydone" % (l, t))
            for fc in range(4):
                ps, pb = psum("misc")
                psv = ps[:].bitcast(BF16)
                k.grp("pe", [Yt_b, cstb_b], [pb], [
                    (lambda t_=t_, fc=fc, psv=psv: nc.tensor.transpose(psv[:, JJ * t_:JJ * (t_ + 1)], Yt[:, t_, 8 * fc:8 * (fc + 1), :], ident_b[0:JJ, 0:JJ]))
                    for t_ in range(8)])
                k.op("act", [pb], [yc0_b[fc]], lambda fc=fc, psv=psv: nc.scalar.copy(
                    out=yc0[:, fc, :].rearrange("p (j t) -> p t j", t=8), in_=psv[:, 0:8 * JJ].rearrange("p (t j) -> p t j", t=8)))
            k.mark("L%d_t%d_s5g_trdone" % (l, t))
            dbg("s5", yc0[:], yc0_b)
            for fc in range(4):
                k.op("act", [yc0_b[fc]], [yc0_b[fc]], lambda fc=fc: nc.scalar.activation(out=yc0[:, fc, :], in_=yc0[:, fc, :], func=AF.Gelu))
                k.op("dve", [yc0_b[fc]], [yc0_b[fc]], lambda fc=fc: nc.vector.tensor_copy(out=yc0b[:, fc, :], in_=yc0[:, fc, :]))
            wt, wb = wload(l, "glu")
            wv = wt[:, 0:4 * 512].rearrange("p (k n) -> p k n", k=4)
            for fc in range(4):
                ps, pb = psum("proj")
                k.grp("pe", [wb] + yc0_b, [pb], [
                    (lambda ps=ps, kc=kc, fc=fc: nc.tensor.matmul(ps[:, 0:TT], lhsT=wv[:, kc, 128 * fc:128 * (fc + 1)], rhs=yc0b[:, kc, :], start=(kc == 0), stop=(kc == 3)))
                    for kc in range(4)])
                mi = rot("mg")
                k.op("act", [pb], [mg_b[mi]], lambda ps=ps, mi=mi: nc.scalar.activation(out=mg[mi][:], in_=ps[:, 0:TT], func=AF.Sigmoid))
                k.op("dve", [mg_b[mi], yc0_b[fc]], [mg_b[mi]], lambda mi=mi, fc=fc: nc.vector.tensor_tensor(out=mg[mi][:], in0=mg[mi][:], in1=yc0[:, fc, :], op=ALU.mult))
                k.op("dve", [mg_b[mi], czs_b[fc]], [ycT_b[fc]], lambda mi=mi, fc=fc: nc.vector.tensor_tensor(out=ycT[:, fc, :], in0=mg[mi][:], in1=czs[:, fc, :], op=ALU.mult))
            dbg("ycT", ycT[:], ycT_b)

        def merge_stage(l, t):
            p = P[l]
            for j in range(8):
                wtg, wbg = wload(l, "g%d" % j)
                wtb, wbb = wload(l, "br%d" % j)
                si = rot("sg")
                for i in range(3):
                    ps, pb = proj_fm(wtg[:, 1024 * i:1024 * (i + 1)], wbg, 128, 0, 128)
                    k.op("act", [pb], [sg_b[si]], lambda ps=ps, si=si, i=i: nc.scalar.activation(out=sg[si][:, i, :], in_=ps[:, 0:TT], func=AF.Sigmoid))
                mi = rot("mg")
                off = 0
                for i, (src, src_bufs, kch) in enumerate([(yaT, [yaT_b], 6), (ybT, [ybT_b], 6), (ycT, ycT_b, 4)]):
                    wv = wtb[:, off:off + kch * 128].rearrange("p (k n) -> p k n", k=kch)
                    off += kch * 128
                    ps, pb = psum("proj")
                    k.grp("pe", [wbb] + src_bufs, [pb], [
                        (lambda ps=ps, kc=kc, wv=wv, src=src, kch=kch: nc.tensor.matmul(ps[:, 0:TT], lhsT=wv[:, kc, :], rhs=src[:, kc, :], start=(kc == 0), stop=(kc == kch - 1)))
                        for kc in range(kch)])
                    if i == 0:
                        k.op("dve", [pb, sg_b[si]], [mg_b[mi]], lambda ps=ps, si=si, mi=mi, i=i: nc.vector.tensor_tensor(out=mg[mi][:], in0=ps[:, 0:TT], in1=sg[si][:, i, :], op=ALU.mult))
                    else:
                        k.op("dve", [pb, sg_b[si]], [sg_b[si]], lambda ps=ps, si=si, i=i: nc.vector.tensor_tensor(out=sg[si][:, i, :], in0=ps[:, 0:TT], in1=sg[si][:, i, :], op=ALU.mult))
                        if i == 1:
                            k.op("pool", [sg_b[si], mg_b[mi]], [mg_b[mi]], lambda si=si, mi=mi, i=i: nc.gpsimd.tensor_tensor(out=mg[mi][:], in0=mg[mi][:], in1=sg[si][:, i, :], op=ALU.add))
                        else:
                            k.op("pool", [sg_b[si], mg_b[mi]], [mT_b[j]], lambda si=si, mi=mi, i=i, j=j: nc.gpsimd.tensor_tensor(out=mT[:, j, :], in0=mg[mi][:], in1=sg[si][:, i, :], op=ALU.add))
            dbg("mT", mT[:], mT_b)
            for h2 in range(4):
                wt, wb = wload(l, "wo%d" % h2)
                wv = wt[:, 0:8 * 256].rearrange("p (k n) -> p k n", k=8)
                for jj in range(2):
                    j = 2 * h2 + jj
                    ps, pb = psum("proj")
                    k.grp("pe", [wb] + mT_b, [pb], [
                        (lambda ps=ps, kc=kc, jj=jj, wv=wv: nc.tensor.matmul(ps[:, 0:TT], lhsT=wv[:, kc, 128 * jj:128 * (jj + 1)], rhs=mT[:, kc, :], start=(kc == 0), stop=(kc == 7)))
                        for kc in range(8)])
                    k.op("dve", [pb, x_b], [x_b], lambda ps=ps, j=j: nc.vector.tensor_tensor(out=x_sb[:, j, :], in0=x_sb[:, j, :], in1=ps[:, 0:TT], op=ALU.add))

        xT_v = xT_d.rearrange("(k p) t -> p k t", p=128)
        oT_v = outT_d.rearrange("(k p) t -> p k t", p=128)
        out_b = Buf("out")
        for t in range(NT):
            k.dma("sp", [s5t["b"]] if "b" in s5t else [], [x_b], lambda t=t: nc.sync.dma_start(out=x_sb[:], in_=xT_v[:, :, t * TT:(t + 1) * TT]))
            for l in range(NL):
                layer(l, t)
            dbg("xo", x_sb[:], [x_b])
            norm_stage(0, None, None)
            for kk in range(8):
                oi = 0
                k.op("dve", [x_b, rstd_b, fnw_b], [o_b[oi]], lambda kk=kk, oi=oi: nc.vector.scalar_tensor_tensor(
                    out=o_sb[oi][:], in0=x_sb[:, kk, :], scalar=fnw[:, kk:kk + 1], in1=rstd[:], op0=ALU.mult, op1=ALU.mult))
                k.dma("sp", [o_b[oi]], [out_b], lambda t=t, kk=kk, oi=oi: nc.sync.dma_start(out=oT_v[:, kk, t * TT:(t + 1) * TT], in_=o_sb[oi][:]))
        k.final_wait("pool", [out_b] + dbg_bufs)
        print("instructions:", k.n_inst, "counts", k.cnt)
        build.marks = k.marks
    return nc


_NC_CACHE = {}


def kernel(**inputs):
    x = np.asarray(inputs["x"], dtype=np.float32)
    B, T, _ = x.shape
    if T not in _NC_CACHE:
        _NC_CACHE[T] = build(T, NL=2, stage=4)
    nc = _NC_CACHE[T]
    bm = host_masks()
    cst = host_consts()
    in_maps = []
    for b in range(B):
        im = {n: np.ascontiguousarray(np.asarray(inputs[n], dtype=np.float32)) for n in WSHAPES}
        im["xT"] = np.ascontiguousarray(x[b].T)
        im["consts"] = cst
        im["bmask"] = bm
        in_maps.append(im)
    res = run_bass_kernel_spmd(nc, in_maps, core_ids=list(range(B)))
    out = np.stack([np.ascontiguousarray(res.results[b]["outT"].T) for b in range(B)], axis=0)
    return out.astype(np.float32)
```
